# Optimizing a Trainium2 kernel written in Bass

```python
import jax
import jax.numpy as jnp
from jax import lax
import numpy as np

D_MODEL = 1024
BATCH = 4
SEQ = 8192
DEPTH = 4

HEAD_DIM = 64
Q_BLOCK = 128
NORM_EPS = 1e-6
TINY = 1e-30
SB_HEADS = 8
MLA_HEADS = 8
MLA_Q_RANK = 256
MLA_KV_RANK = 128
MLA_NOPE = 64
MLA_ROPE = 32
MLA_V = 64
ROPE_BASE = 10000.0
NSA_HEADS = 12
NSA_KV_HEADS = 3
NSA_HPG = NSA_HEADS // NSA_KV_HEADS
NSA_Q_BLOCK = 64
CMP_LEN = 32
CMP_STRIDE = 16
SEL_LEN = 64
SEL_TOPN = 16
WIN = 512
FORCE_BONUS = 1e3
DIL_CFG = ((128, 1), (512, 4), (2048, 16))
N_DIL = len(DIL_CFG)
DIL_HEADS = 4
SB_W = SB_HEADS * HEAD_DIM
MLA_OUT = MLA_HEADS * MLA_V
NSA_W = NSA_HEADS * HEAD_DIM
NSA_KV_W = NSA_KV_HEADS * HEAD_DIM
DIL_W = DIL_HEADS * HEAD_DIM
EVEN_SPLITS = (SB_W, SB_W, SB_W, SB_W, MLA_Q_RANK, MLA_KV_RANK, MLA_ROPE, MLA_OUT)
ODD_SPLITS = (NSA_W,) + (NSA_KV_W,) * 6 + (3 * NSA_HEADS, NSA_W) + (N_DIL * DIL_W,) * 3 + (DIL_W,)
EVEN_IN = sum(EVEN_SPLITS)
ODD_IN = sum(ODD_SPLITS)
EVEN_MIX = SB_W + MLA_OUT
ODD_MIX = NSA_W + DIL_W

kernel_name = 'hybrid_stickbreak_mla_nsa_dilated_adaln'


def _split(t, widths):
    return jnp.split(t, np.cumsum(widths)[:-1].tolist(), axis=-1)


def _rms(t, g):
    tf = t.astype(jnp.float32)
    y = tf * lax.rsqrt(jnp.mean(tf * tf, axis=-1, keepdims=True) + NORM_EPS)
    return (y * g.astype(jnp.float32)).astype(t.dtype)


def _rope(t, cos, sin):
    t1, t2 = jnp.split(t.astype(jnp.float32), 2, axis=-1)
    return jnp.concatenate([t1 * cos - t2 * sin, t1 * sin + t2 * cos], axis=-1).astype(t.dtype)


def _alibi_slopes(n):
    return 2.0 ** (-8.0 * jnp.arange(1, n + 1, dtype=jnp.float32) / n)


def _masked_softmax(s, mask):
    s = jnp.where(mask, s, -jnp.inf)
    mx = jnp.max(s, axis=-1, keepdims=True)
    mx = jnp.where(jnp.isfinite(mx), mx, 0.0)
    e = jnp.exp(s - mx)
    den = jnp.maximum(jnp.sum(e, axis=-1, keepdims=True), TINY)
    return e / den, (mx + jnp.log(den))[..., 0]


def _sweep(fn, seq, blk):
    out = lax.map(fn, jnp.arange(seq // blk))
    out = jnp.moveaxis(out, 0, 1)
    return out.reshape((out.shape[0], seq) + out.shape[3:])


def _modulate(x, c, ada_w, ada_b, norm_g):
    mod = jax.nn.silu(c) @ ada_w + ada_b
    shift, scale, gate = jnp.split(mod, 3, axis=-1)
    h = _rms(x, norm_g) * (1.0 + scale[:, None, :]) + shift[:, None, :]
    return h, gate[:, None, :]


def _stick_breaking(q, k, v):
    B, S, H, Dh = q.shape
    scale = Dh ** -0.5
    kpos = jnp.arange(S)

    def block(i):
        q0 = i * Q_BLOCK
        qb = lax.dynamic_slice_in_dim(q, q0, Q_BLOCK, axis=1)
        z = jnp.einsum('bqhd,bkhd->bhqk', qb, k, preferred_element_type=jnp.float32) * scale
        strict = kpos[None, :] < (q0 + jnp.arange(Q_BLOCK))[:, None]
        log_beta = jax.nn.log_sigmoid(z)
        log_one_minus = jnp.where(strict, log_beta - z, 0.0)
        tail = lax.cumsum(log_one_minus, axis=3, reverse=True) - log_one_minus
        w = jnp.where(strict, jnp.exp(log_beta + tail), 0.0)
        return jnp.einsum('bhqk,bkhd->bqhd', w, v)

    return _sweep(block, S, Q_BLOCK)


def _causal_softmax_attn(q, k, v, scale):
    B, S = q.shape[:2]
    kpos = jnp.arange(S)

    def block(i):
        q0 = i * Q_BLOCK
        qb = lax.dynamic_slice_in_dim(q, q0, Q_BLOCK, axis=1)
        s = jnp.einsum('bqhd,bkhd->bhqk', qb, k, preferred_element_type=jnp.float32) * scale
        p, _ = _masked_softmax(s, kpos[None, :] <= (q0 + jnp.arange(Q_BLOCK))[:, None])
        return jnp.einsum('bhqk,bkhd->bqhd', p, v)

    return _sweep(block, S, Q_BLOCK)


def _mla(q_lat, kv_lat, k_rope, qa_g, wq_up, kva_g, wkv_up, qn, kn, cos, sin):
    B, S, _ = q_lat.shape
    q = (_rms(q_lat, qa_g) @ wq_up).reshape(B, S, MLA_HEADS, MLA_NOPE + MLA_ROPE)
    kv = (_rms(kv_lat, kva_g) @ wkv_up).reshape(B, S, MLA_HEADS, MLA_NOPE + MLA_V)
    q_nope = _rms(q[..., :MLA_NOPE], qn[:MLA_NOPE])
    q_rot = _rope(_rms(q[..., MLA_NOPE:], qn[MLA_NOPE:]), cos[:, None, :], sin[:, None, :])
    k_nope = _rms(kv[..., :MLA_NOPE], kn[:MLA_NOPE])
    k_rot = _rope(_rms(k_rope, kn[MLA_NOPE:]), cos, sin)
    v = kv[..., MLA_NOPE:]
    qf = jnp.concatenate([q_nope, q_rot], axis=-1)
    kf = jnp.concatenate([k_nope, jnp.broadcast_to(k_rot[:, :, None, :], (B, S, MLA_HEADS, MLA_ROPE))], axis=-1)
    return _causal_softmax_attn(qf, kf, v, (MLA_NOPE + MLA_ROPE) ** -0.5)


def _compress(t, cidx, pe, w):
    blocks = t[:, cidx] + pe[:, None, :]
    return jnp.einsum('bnlgd,lde->bnge', blocks, w.reshape(CMP_LEN, HEAD_DIM, HEAD_DIM))


def _nsa(q, kc, vc, ks, vs, kw, vw, gates, cpos, cend, pos_f, slopes):
    B, S, G, HPG, Dh = q.shape
    scale = Dh ** -0.5
    n_sel = S // SEL_LEN
    topn = min(SEL_TOPN, n_sel)
    cstart = cend - (CMP_LEN - 1)
    jstart = jnp.arange(n_sel) * SEL_LEN
    overlap = ((cstart[:, None] <= jstart[None, :] + SEL_LEN - 1)
               & (cend[:, None] >= jstart[None, :])).astype(jnp.float32)
    kb = ks.reshape(B, n_sel, SEL_LEN, G, Dh).transpose(0, 3, 1, 2, 4)
    vb = vs.reshape(B, n_sel, SEL_LEN, G, Dh).transpose(0, 3, 1, 2, 4)
    kpad = jnp.pad(kw, ((0, 0), (WIN, 0), (0, 0), (0, 0)))
    vpad = jnp.pad(vw, ((0, 0), (WIN, 0), (0, 0), (0, 0)))
    ppad = jnp.pad(pos_f, (WIN, 0))
    sl = slopes.reshape(1, G, HPG, 1, 1)
    bi = jnp.arange(B)[:, None, None, None]
    gi = jnp.arange(G)[None, :, None, None]
    jblk = jnp.arange(n_sel)
    wrel = jnp.arange(WIN + NSA_Q_BLOCK) - WIN
    n_keys = topn * SEL_LEN

    def block(i):
        q0 = i * NSA_Q_BLOCK
        tq = q0 + jnp.arange(NSA_Q_BLOCK)
        pq = lax.dynamic_slice_in_dim(pos_f, q0, NSA_Q_BLOCK)
        qb = lax.dynamic_slice_in_dim(q, q0, NSA_Q_BLOCK, axis=1)
        gb = lax.dynamic_slice_in_dim(gates, q0, NSA_Q_BLOCK, axis=1)
        s = (jnp.einsum('bqghd,bngd->bghqn', qb, kc, preferred_element_type=jnp.float32) * scale
             - sl * (pq[:, None] - cpos[None, :]))
        p_c, _ = _masked_softmax(s, cend[None, :] <= tq[:, None])
        o_c = jnp.einsum('bghqn,bngd->bqghd', p_c, vc)
        imp = jnp.einsum('bghqn,nj->bgqj', p_c, overlap)
        cur = tq[:, None] // SEL_LEN
        forced = (jblk[None, :] == 0) | (jblk[None, :] == cur) | (jblk[None, :] == cur - 1)
        score = jnp.where(jblk[None, :] <= cur, imp + FORCE_BONUS * forced, -jnp.inf)
        top_val, top_idx = lax.top_k(score, topn)
        tok = top_idx[..., None] * SEL_LEN + jnp.arange(SEL_LEN)
        m_s = jnp.isfinite(top_val)[..., None] & (tok <= tq[:, None, None])
        tok = tok.reshape(B, G, NSA_Q_BLOCK, n_keys)
        m_s = m_s.reshape(B, G, NSA_Q_BLOCK, n_keys)
        k_g = kb[bi, gi, top_idx].reshape(B, G, NSA_Q_BLOCK, n_keys, Dh)
        v_g = vb[bi, gi, top_idx].reshape(B, G, NSA_Q_BLOCK, n_keys, Dh)
        s = (jnp.einsum('bqghd,bgqmd->bghqm', qb, k_g, preferred_element_type=jnp.float32) * scale
             - sl * (pq[:, None] - pos_f[tok])[:, :, None])
        p_s, _ = _masked_softmax(s, m_s[:, :, None])
        o_s = jnp.einsum('bghqm,bgqmd->bqghd', p_s, v_g)
        kwb = lax.dynamic_slice_in_dim(kpad, q0, WIN + NSA_Q_BLOCK, axis=1)
        vwb = lax.dynamic_slice_in_dim(vpad, q0, WIN + NSA_Q_BLOCK, axis=1)
        pwb = lax.dynamic_slice_in_dim(ppad, q0, WIN + NSA_Q_BLOCK)
        kidx = q0 + wrel
        dist = tq[:, None] - kidx[None, :]
        m_w = (kidx[None, :] >= 0) & (dist >= 0) & (dist < WIN)
        s = (jnp.einsum('bqghd,bkgd->bghqk', qb, kwb, preferred_element_type=jnp.float32) * scale
             - sl * (pq[:, None] - pwb[None, :]))
        p_w, _ = _masked_softmax(s, m_w)
        o_w = jnp.einsum('bghqk,bkgd->bqghd', p_w, vwb)
        return gb[..., 0:1] * o_c + gb[..., 1:2] * o_s + gb[..., 2:3] * o_w

    return _sweep(block, S, NSA_Q_BLOCK)


def _dilated(q, k, v, pos_f, slopes):
    B, S, _, H, Dh = q.shape
    scale = Dh ** -0.5
    sl = slopes.reshape(N_DIL, H)
    k_groups = [k[:, :, g] for g in range(N_DIL)]
    v_groups = [v[:, :, g] for g in range(N_DIL)]

    def block(i):
        q0 = i * Q_BLOCK
        tq = q0 + jnp.arange(Q_BLOCK)
        pq = lax.dynamic_slice_in_dim(pos_f, q0, Q_BLOCK)
        qb = lax.dynamic_slice_in_dim(q, q0, Q_BLOCK, axis=1)
        outs, lses = [], []
        for g, (window, dil) in enumerate(DIL_CFG):
            kidx = tq[:, None] - dil * jnp.arange(window // dil + 1)[None, :]
            valid = kidx >= 0
            kidx = jnp.maximum(kidx, 0)
            s = (jnp.einsum('bqhd,bqkhd->bhqk', qb[:, :, g], k_groups[g][:, kidx],
                            preferred_element_type=jnp.float32) * scale
                 - sl[g][:, None, None] * (pq[:, None] - pos_f[kidx]))
            p, lse = _masked_softmax(s, valid)
            outs.append(jnp.einsum('bhqk,bqkhd->bqhd', p, v_groups[g][:, kidx]))
            lses.append(lse)
        alpha = jax.nn.softmax(jnp.stack(lses), axis=0)
        return jnp.einsum('gbhq,gbqhd->bqhd', alpha, jnp.stack(outs))

    return _sweep(block, S, Q_BLOCK)


def _even_mixer(h, w_in, w_out, sb_qn, sb_kn, qa_g, wq_up, kva_g, wkv_up, qn, kn, cos, sin):
    B, S, _ = h.shape
    sb_q, sb_k, sb_v, sb_z, q_lat, kv_lat, k_rope, mla_z = _split(h @ w_in, EVEN_SPLITS)
    heads = lambda t: t.reshape(B, S, SB_HEADS, HEAD_DIM)
    o_sb = _stick_breaking(_rms(heads(sb_q), sb_qn), _rms(heads(sb_k), sb_kn), heads(sb_v))
    o_mla = _mla(q_lat, kv_lat, k_rope, qa_g, wq_up, kva_g, wkv_up, qn, kn, cos, sin)
    mixed = jnp.concatenate([o_sb.reshape(B, S, SB_W) * jax.nn.silu(sb_z),
                             o_mla.reshape(B, S, MLA_OUT) * jax.nn.silu(mla_z)], axis=-1)
    return mixed @ w_out


def _odd_mixer(h, w_in, w_out, nsa_qn, nsa_kn, cmp_wk, cmp_wv, pe_k, pe_v, dil_qn, dil_kn,
               pos_f, cidx, cend, cpos, nsa_slopes, dil_slopes):
    B, S, _ = h.shape
    nq, ck, cv, sk, sv, wk, wv, ng, nz, dq, dk, dv, dz = _split(h @ w_in, ODD_SPLITS)
    kvh = lambda t: t.reshape(B, S, NSA_KV_HEADS, HEAD_DIM)
    q = _rms(nq.reshape(B, S, NSA_KV_HEADS, NSA_HPG, HEAD_DIM), nsa_qn)
    kc = _rms(_compress(kvh(ck), cidx, pe_k, cmp_wk), nsa_kn)
    vc = _compress(kvh(cv), cidx, pe_v, cmp_wv)
    gates = jax.nn.sigmoid(ng.astype(jnp.float32)).reshape(B, S, NSA_KV_HEADS, NSA_HPG, 3)
    o_nsa = _nsa(q, kc, vc, _rms(kvh(sk), nsa_kn), kvh(sv), _rms(kvh(wk), nsa_kn), kvh(wv),
                 gates, cpos, cend, pos_f, nsa_slopes)
    dh = lambda t: t.reshape(B, S, N_DIL, DIL_HEADS, HEAD_DIM)
    o_dil = _dilated(_rms(dh(dq), dil_qn), _rms(dh(dk), dil_kn), dh(dv), pos_f, dil_slopes)
    mixed = jnp.concatenate([o_nsa.reshape(B, S, NSA_W) * jax.nn.silu(nz),
                             o_dil.reshape(B, S, DIL_W) * jax.nn.silu(dz)], axis=-1)
    return mixed @ w_out


def setup_inputs(seed: int = 0) -> dict:
    key = jax.random.key(seed)
    keys = iter(jax.random.split(key, 32))

    def dense(shape, fan_in, gain=1.0):
        return gain * fan_in ** -0.5 * jax.random.normal(next(keys), shape, jnp.float32)

    def norm_gain(shape):
        return 1.0 + 0.05 * jax.random.normal(next(keys), shape, jnp.float32)

    n_even = (DEPTH + 1) // 2
    n_odd = DEPTH // 2
    D = D_MODEL
    return {
        'x': jax.random.normal(next(keys), (BATCH, SEQ, D), jnp.float32),
        'c': jax.random.normal(next(keys), (BATCH, D), jnp.float32),
        'positions': jnp.arange(SEQ, dtype=jnp.int32),
        'ada_w': dense((DEPTH, D, 3 * D), D, 0.5),
        'ada_b': 0.02 * jax.random.normal(next(keys), (DEPTH, 3 * D), jnp.float32),
        'norm_g': norm_gain((DEPTH, D)),
        'ev_w_in': dense((n_even, D, EVEN_IN), D),
        'ev_w_out': dense((n_even, EVEN_MIX, D), EVEN_MIX),
        'sb_qn': norm_gain((n_even, HEAD_DIM)),
        'sb_kn': norm_gain((n_even, HEAD_DIM)),
        'mla_qa_g': norm_gain((n_even, MLA_Q_RANK)),
        'mla_wq_up': dense((n_even, MLA_Q_RANK, MLA_HEADS * (MLA_NOPE + MLA_ROPE)), MLA_Q_RANK),
        'mla_kva_g': norm_gain((n_even, MLA_KV_RANK)),
        'mla_wkv_up': dense((n_even, MLA_KV_RANK, MLA_HEADS * (MLA_NOPE + MLA_V)), MLA_KV_RANK),
        'mla_qn': norm_gain((n_even, MLA_NOPE + MLA_ROPE)),
        'mla_kn': norm_gain((n_even, MLA_NOPE + MLA_ROPE)),
        'od_w_in': dense((n_odd, D, ODD_IN), D),
        'od_w_out': dense((n_odd, ODD_MIX, D), ODD_MIX),
        'nsa_qn': norm_gain((n_odd, HEAD_DIM)),
        'nsa_kn': norm_gain((n_odd, HEAD_DIM)),
        'nsa_cmp_wk': dense((n_odd, CMP_LEN * HEAD_DIM, HEAD_DIM), CMP_LEN * HEAD_DIM),
        'nsa_cmp_wv': dense((n_odd, CMP_LEN * HEAD_DIM, HEAD_DIM), CMP_LEN * HEAD_DIM),
        'nsa_cmp_pe_k': 0.1 * jax.random.normal(next(keys), (n_odd, CMP_LEN, HEAD_DIM), jnp.float32),
        'nsa_cmp_pe_v': 0.1 * jax.random.normal(next(keys), (n_odd, CMP_LEN, HEAD_DIM), jnp.float32),
        'dil_qn': norm_gain((n_odd, HEAD_DIM)),
        'dil_kn': norm_gain((n_odd, HEAD_DIM)),
    }


def reference(x, c, positions, ada_w, ada_b, norm_g, ev_w_in, ev_w_out, sb_qn, sb_kn,
              mla_qa_g, mla_wq_up, mla_kva_g, mla_wkv_up, mla_qn, mla_kn, od_w_in, od_w_out,
              nsa_qn, nsa_kn, nsa_cmp_wk, nsa_cmp_wv, nsa_cmp_pe_k, nsa_cmp_pe_v, dil_qn, dil_kn):
    S = x.shape[1]
    pos_f = positions.astype(jnp.float32)
    inv_freq = ROPE_BASE ** (-jnp.arange(0, MLA_ROPE, 2, dtype=jnp.float32) / MLA_ROPE)
    ang = pos_f[:, None] * inv_freq[None, :]
    cos, sin = jnp.cos(ang), jnp.sin(ang)
    nsa_slopes = _alibi_slopes(NSA_HEADS)
    dil_slopes = _alibi_slopes(N_DIL * DIL_HEADS)
    n_cmp = (S - CMP_LEN) // CMP_STRIDE + 1
    cidx = jnp.arange(n_cmp)[:, None] * CMP_STRIDE + jnp.arange(CMP_LEN)[None, :]
    cend = cidx[:, -1]
    cpos = jnp.mean(pos_f[cidx], axis=-1)
    for layer in range(DEPTH):
        h, gate = _modulate(x, c, ada_w[layer], ada_b[layer], norm_g[layer])
        j = layer // 2
        if layer % 2 == 0:
            y = _even_mixer(h, ev_w_in[j], ev_w_out[j], sb_qn[j], sb_kn[j], mla_qa_g[j], mla_wq_up[j],
                            mla_kva_g[j], mla_wkv_up[j], mla_qn[j], mla_kn[j], cos, sin)
        else:
            y = _odd_mixer(h, od_w_in[j], od_w_out[j], nsa_qn[j], nsa_kn[j], nsa_cmp_wk[j], nsa_cmp_wv[j],
                           nsa_cmp_pe_k[j], nsa_cmp_pe_v[j], dil_qn[j], dil_kn[j],
                           pos_f, cidx, cend, cpos, nsa_slopes, dil_slopes)
        x = (x + gate * y).astype(x.dtype)
    return x
```

```python
import math
import numpy as np
import ml_dtypes
import concourse.bass as bass
import concourse.mybir as mybir
from concourse.bass_utils import run_bass_kernel_spmd
from contextlib import ExitStack

F32 = mybir.dt.float32
BF16 = mybir.dt.bfloat16
I32 = mybir.dt.int32
AF = mybir.ActivationFunctionType
ALU = mybir.AluOpType
NPBF = ml_dtypes.bfloat16

D = 1024
HD = 64
EPS = 1e-6
NEG = -30000.0
TWO_PI = 2.0 * math.pi

EVEN_IN = 2976
ODD_IN = 5284

ENGS = ("pe", "act", "dve", "pool", "sp")
NDMA = 12


class Buf:
    __slots__ = ("name", "w", "r")

    def __init__(self, name=""):
        self.name = name
        self.w = None
        self.r = []


class _Rec:
    def __getattr__(self, name):
        def f(*a, **kw):
            self.call = (name, a, kw)
            return self
        return f


class KB:
    def __init__(self, nc, st):
        self.nc = nc
        self.st = st
        self.esem = {e: st.enter_context(nc.semaphore("s_" + e)) for e in ENGS}
        self.ecnt = {e: 0 for e in ENGS}
        self.dsem = [st.enter_context(nc.semaphore("d%d" % i)) for i in range(NDMA)]
        self.dcnt = [0] * NDMA
        self.dnext = 0
        self.stream = {e: [] for e in ENGS}
        self.known = {e: {} for e in ENGS}
        self.sems = {}
        for e in ENGS:
            self.sems[e] = self.esem[e]
        for i in range(NDMA):
            self.sems["d%d" % i] = self.dsem[i]
        self.nbuf = 0
        self.nt = 0

    def sb(self, name, shape, dt):
        return self.st.enter_context(self.nc.sbuf_tensor("sb_" + name, list(shape), dt))

    def ps(self, name, shape=(128, 512), dt=F32):
        return self.st.enter_context(self.nc.psum_tensor("ps_" + name, list(shape), dt))

    def buf(self, name=""):
        self.nbuf += 1
        return Buf(name or "b%d" % self.nbuf)

    def tile(self, shape, dt, name=None):
        self.nt += 1
        return self.sb(name or ("t%d" % self.nt), shape, dt), self.buf()

    def _waits(self, eng, reads, writes, chain):
        need = {}

        def add(tok, raw):
            if tok is None:
                return
            key, val, src = tok
            if src == eng and not key.startswith("d"):
                if not raw or chain:
                    return
            if self.known[eng].get(key, 0) >= val:
                return
            if need.get(key, 0) < val:
                need[key] = val

        for b in reads:
            add(b.w, True)
        for b in writes:
            add(b.w, False)
            for t in b.r:
                add(t, False)
        for key, val in need.items():
            self.known[eng][key] = val
        return list(need.items())

    def op(self, eng, thunk, r=(), w=(), chain=False):
        rec = _Rec()
        thunk(rec)
        name_, a_, kw_ = rec.call
        thunk = lambda e: getattr(e, name_)(*a_, **kw_)
        waits = self._waits(eng, r, w, chain)
        self.ecnt[eng] += 1
        tok = (eng, self.ecnt[eng], eng)
        self.stream[eng].append((waits, thunk, (eng, 1)))
        for b in r:
            b.r.append(tok)
        for b in w:
            b.w = tok
            b.r = []
        return tok

    def dma(self, out, in_, r=(), w=(), q="sp", **kw):
        i = self.dnext
        self.dnext = (self.dnext + 1) % NDMA
        key = "d%d" % i
        waits = self._waits(q, r, w, False)
        prev = self.dcnt[i]
        if prev > 0 and self.known[q].get(key, 0) < prev:
            waits.append((key, prev))
            self.known[q][key] = prev
        self.dcnt[i] += 16
        tok = (key, self.dcnt[i], q)
        self.stream[q].append((waits, lambda e: e.dma_start(out=out, in_=in_, **kw), (key, 16)))
        for b in r:
            b.r.append(tok)
        for b in w:
            b.w = tok
            b.r = []
        return tok

    def finish(self):
        nc = self.nc
        final_waits = [("d%d" % i, self.dcnt[i]) for i in range(NDMA) if self.dcnt[i] > 0]
        block = self.st.enter_context(nc.Block())
        sems = self.sems
        stream = self.stream

        def emit(eng_name):
            def f(e):
                for waits, thunk, inc in stream[eng_name]:
                    for key, val in waits:
                        e.wait_ge(sems[key], val)
                    ins = thunk(e)
                    ins.then_inc(sems[inc[0]], inc[1])
                if eng_name == "sp":
                    for key, val in final_waits:
                        e.wait_ge(sems[key], val)
            return f

        block.tensor(emit("pe"))
        block.scalar(emit("act"))
        block.vector(emit("dve"))
        block.gpsimd(emit("pool"))
        block.sync(emit("sp"))


class Rot:
    def __init__(self, k, name, shape, dt, n, psum=False):
        self.items = []
        for i in range(n):
            if psum:
                t = k.ps("%s%d" % (name, i), shape, dt)
            else:
                t = k.sb("%s%d" % (name, i), shape, dt)
            self.items.append((t, k.buf("%s%d" % (name, i))))
        self.i = 0

    def next(self):
        it = self.items[self.i]
        self.i = (self.i + 1) % len(self.items)
        return it


def _blockdiag(sizes, n=128):
    m = np.zeros((n, n), np.float32)
    o = 0
    for s in sizes:
        m[o:o + s, o:o + s] = 1.0 / s
        o += s
    return m


def _rot96():
    m = np.zeros((96, 96), np.float32)
    for i in range(64, 80):
        m[i + 16, i] = -1.0
    for i in range(80, 96):
        m[i - 16, i] = 1.0
    return m


def _invf96():
    inv = (10000.0 ** (-np.arange(0, 32, 2, dtype=np.float32) / np.float32(32))).astype(np.float32)
    v = np.zeros((96, 1), np.float32)
    v[64:80, 0] = inv
    v[80:96, 0] = inv
    return v


EVEN_FMB = 512 + 512 + 768 + 512 + 32
EVEN_FMF = 1024
EVEN_TM = 1024
ODD_FMB = 768 + 384 + 768 + 768 + 384
ODD_FMF = 768 + 256 + 36
ODD_TM = 192 + 192 + 768


def build_proj(Tc, prev, nxt, debug=False):
    nc = bass.Bass("TRN2", target_bir_lowering=False)
    NCH = Tc // 512
    dr = {}

    def din(name, shape, dt=F32):
        dr[name] = nc.dram_tensor(name, list(shape), dt, kind="ExternalInput").ap()
        return dr[name]

    def dout(name, shape, dt=F32):
        dr[name] = nc.dram_tensor(name, list(shape), dt, kind="ExternalOutput").ap()
        return dr[name]

    xT = din("xT", [D, Tc])
    c_col = din("c_col", [128, 8])
    cmat = din("cmat", [128, 5 * 128])
    if prev:
        mixT = din("mixT", [D, Tc], BF16)
        w_out = din("w_out", [D, D])
        ada_wg = din("ada_wg", [D, D])
        ada_bg = din("ada_bg", [128, 8])
        xT_new = dout("xT_new", [D, Tc])
    if nxt:
        NIN = EVEN_IN if nxt == "even" else ODD_IN
        w_in = din("w_in", [D, NIN])
        ada_ws = din("ada_ws", [D, 2 * D])
        ada_bs = din("ada_bs", [128, 16])
        normg = din("normg", [128, 8])
        g128 = din("g128", [128, 4])
        nfmb, nfmf, ntm = (EVEN_FMB, EVEN_FMF, EVEN_TM) if nxt == "even" else (ODD_FMB, ODD_FMF, ODD_TM)
        fmb = dout("fmb", [nfmb, Tc], BF16)
        fmf = dout("fmf", [nfmf, Tc])
        tmb = dout("tmb", [Tc, ntm], BF16)
        if nxt == "even":
            wq_up = din("wq_up", [256, 768])
            wkv_up = din("wkv_up", [128, 1024])
            g96 = din("g96", [96, 2])
            qag = din("qag", [128, 2])
            kvag = din("kvag", [128, 1])
            pos = din("pos", [Tc], I32)
            rotm = din("rotm", [96, 96])
            invf = din("invf", [96, 1])

    with ExitStack() as st:
        k = KB(nc, st)
        cm_f, b_cmf = k.tile([128, 640], F32, "cm_f")
        cm_b, b_cm = k.tile([128, 640], BF16, "cm_b")
        k.dma(cm_f[:], cmat[:, :], w=[b_cmf])
        k.op("dve", lambda e: e.tensor_copy(out=cm_b[:], in_=cm_f[:]), r=[b_cmf], w=[b_cm])
        ONES1024 = cm_b[:, 0:128]
        BLK64 = cm_b[:, 128:256]
        ONES256 = cm_b[:, 256:384]
        ONES128 = cm_b[:, 384:512]
        BLK96 = cm_b[0:96, 512:608]
        epsc, b_eps = k.tile([128, 1], F32, "epsc")
        k.op("pool", lambda e: e.memset(epsc[:], EPS), w=[b_eps])

        PS = Rot(k, "ps", [128, 512], F32, 8, psum=True)
        stage = Rot(k, "stg", [128, 1024], F32, 3)

        ccol, b_cc = k.tile([128, 8], F32, "ccol")
        k.dma(ccol[:], c_col[:, :], w=[b_cc])
        scol, b_sc = k.tile([128, 8], F32, "scol")
        k.op("act", lambda e: e.activation(out=scol[:], in_=ccol[:], func=AF.Silu), r=[b_cc], w=[b_sc])

        def ada_cols(wsrc, ncolchunks, name):
            res, b_res = k.tile([128, ncolchunks], F32, name)
            pst, b_ps = PS.next()
            for j in range(ncolchunks):
                for half in range(2):
                    stg, b_stg = stage.next()
                    src = wsrc[half * 512:(half + 1) * 512, j * 128:(j + 1) * 128].rearrange("(i p) n -> p i n", p=128)
                    k.dma(stg[:, 0:512].rearrange("p (i n) -> p i n", i=4), src, w=[b_stg])
                    for ii in range(4):
                        i = half * 4 + ii
                        k.op("pe", (lambda stg=stg, ii=ii, i=i, j=j, pst=pst: lambda e: e.matmul(
                            pst[:, j:j + 1], lhsT=stg[:, ii * 128:(ii + 1) * 128], rhs=scol[:, i:i + 1],
                            start=(i == 0), stop=(i == 7)))(),
                            r=[b_stg, b_sc], w=[b_ps], chain=True)
            k.op("dve", lambda e: e.tensor_copy(out=res[:], in_=pst[:, 0:ncolchunks]), r=[b_ps], w=[b_res])
            return res, b_res

        if prev:
            graw, b_graw = ada_cols(ada_wg, 8, "graw")
            abg, b_abg = k.tile([128, 8], F32, "abg")
            k.dma(abg[:], ada_bg[:, :], w=[b_abg])
            gate, b_gate = k.tile([128, 8], F32, "gate")
            k.op("dve", lambda e: e.tensor_tensor(out=gate[:], in0=graw[:], in1=abg[:], op=ALU.add), r=[b_graw, b_abg], w=[b_gate])
            wo_b, b_wo = k.tile([128, 8 * D], BF16, "wo_b")
            for i in range(8):
                stg, b_stg = stage.next()
                k.dma(stg[:, 0:D], w_out[i * 128:(i + 1) * 128, :], w=[b_stg])
                k.op("dve" if i % 2 else "pool", (lambda stg=stg, i=i: lambda e: e.tensor_copy(out=wo_b[:, i * D:(i + 1) * D], in_=stg[:, 0:D]))(),
                     r=[b_stg], w=[b_wo])
        if nxt:
            ssraw, b_ssraw = ada_cols(ada_ws, 16, "ssraw")
            abs_, b_abs = k.tile([128, 16], F32, "abs_")
            k.dma(abs_[:], ada_bs[:, :], w=[b_abs])
            ng_, b_ng = k.tile([128, 8], F32, "ng_")
            k.dma(ng_[:], normg[:, :], w=[b_ng])
            ss, b_ss = k.tile([128, 16], F32, "ss")
            k.op("dve", lambda e: e.tensor_tensor(out=ss[:], in0=ssraw[:], in1=abs_[:], op=ALU.add), r=[b_ssraw, b_abs], w=[b_ss])
            Acol, b_A = k.tile([128, 8], F32, "Acol")
            k.op("dve", lambda e: e.scalar_tensor_tensor(out=Acol[:], in0=ss[:, 8:16], scalar=1.0, in1=ng_[:], op0=ALU.add, op1=ALU.mult),
                 r=[b_ss, b_ng], w=[b_A])
            w_b, b_w = k.tile([128, 8 * NIN], BF16, "w_b")
            ci = 0
            for i in range(8):
                for c0 in range(0, NIN, 1024):
                    cw = min(1024, NIN - c0)
                    stg, b_stg = stage.next()
                    k.dma(stg[:, 0:cw], w_in[i * 128:(i + 1) * 128, c0:c0 + cw], w=[b_stg])
                    k.op("dve" if ci % 2 else "pool", (lambda stg=stg, i=i, c0=c0, cw=cw: lambda e: e.tensor_copy(
                        out=w_b[:, i * NIN + c0:i * NIN + c0 + cw], in_=stg[:, 0:cw]))(), r=[b_stg], w=[b_w])
                    ci += 1
            gl, b_gl = k.tile([128, 4], F32, "gl")
            k.dma(gl[:], g128[:, :], w=[b_gl])
            gsc, b_gsc = k.tile([128, 4], F32, "gsc")
            for j in range(4):
                k.op("dve", (lambda j=j: lambda e: e.tensor_scalar(out=gsc[:, j:j + 1], in0=gl[:, j:j + 1],
                                                                    scalar1=(0.125 if j % 2 == 0 else 1.0), scalar2=None, op0=ALU.mult))(),
                     r=[b_gl], w=[b_gsc])
            if nxt == "even":
                wq_b, b_wq = k.tile([128, 2 * 768], BF16, "wq_b")
                for i in range(2):
                    stg, b_stg = stage.next()
                    k.dma(stg[:, 0:768], wq_up[i * 128:(i + 1) * 128, :], w=[b_stg])
                    k.op("dve", (lambda stg=stg, i=i: lambda e: e.tensor_copy(out=wq_b[:, i * 768:(i + 1) * 768], in_=stg[:, 0:768]))(), r=[b_stg], w=[b_wq])
                wkv_b, b_wkv = k.tile([128, 1024], BF16, "wkv_b")
                stg, b_stg = stage.next()
                k.dma(stg[:, 0:1024], wkv_up[:, :], w=[b_stg])
                k.op("dve", (lambda stg=stg: lambda e: e.tensor_copy(out=wkv_b[:], in_=stg[:, 0:1024]))(), r=[b_stg], w=[b_wkv])
                g96l, b_g96l = k.tile([96, 2], F32, "g96l")
                k.dma(g96l[:], g96[:, :], w=[b_g96l])
                g96s, b_g96 = k.tile([96, 2], F32, "g96s")
                k.op("dve", lambda e: e.tensor_scalar(out=g96s[:, 0:1], in0=g96l[:, 0:1], scalar1=96.0 ** -0.5, scalar2=None, op0=ALU.mult), r=[b_g96l], w=[b_g96])
                k.op("dve", lambda e: e.tensor_copy(out=g96s[:, 1:2], in_=g96l[:, 1:2]), r=[b_g96l], w=[b_g96])
                qagt, b_qag = k.tile([128, 2], F32, "qagt")
                k.dma(qagt[:], qag[:, :], w=[b_qag])
                kvagt, b_kvag = k.tile([128, 1], F32, "kvagt")
                k.dma(kvagt[:], kvag[:, :], w=[b_kvag])
                rot_f, b_rotf = k.tile([96, 96], F32, "rot_f")
                k.dma(rot_f[:], rotm[:, :], w=[b_rotf])
                rot_b, b_rot = k.tile([96, 96], BF16, "rot_b")
                k.op("dve", lambda e: e.tensor_copy(out=rot_b[:], in_=rot_f[:]), r=[b_rotf], w=[b_rot])
                invft, b_invf = k.tile([96, 1], F32, "invft")
                k.dma(invft[:], invf[:, :], w=[b_invf])

        xch = Rot(k, "xch", [128, 8 * 512], F32, 1)
        if prev:
            mch = Rot(k, "mch", [128, 8 * 512], BF16, 1)
        if nxt:
            sqx = Rot(k, "sqx", [128, 8 * 512], BF16, 1)
            hch = Rot(k, "hch", [128, 8 * 512], BF16, 1)
            tf = Rot(k, "tf", [128, 512], F32, 4)
            tb = Rot(k, "tb", [128, 512], BF16, 4)
            ob = Rot(k, "ob", [128, 512], BF16, 4)
            of = Rot(k, "of", [128, 512], F32, 3)

        rope_t = {}
        if nxt:
            rsx = k.tile([128, 512], F32, "rsx")
        if nxt == "even":
            rope_t["pos_i"] = k.tile([96, 512], I32, "pos_i")
            rope_t["ang"] = k.tile([96, 512], F32, "ang")
            rope_t["a2_0"] = k.tile([96, 512], F32, "a2_0")
            rope_t["a2_1"] = k.tile([96, 512], F32, "a2_1")
            rope_t["kf"] = k.tile([96, 512], F32, "kf")
            rope_t["ki"] = k.tile([96, 512], I32, "ki")
            rope_t["qln0"] = k.tile([128, 512], BF16, "qln0")
            rope_t["qln1"] = k.tile([128, 512], BF16, "qln1")
            rope_t["kvn"] = k.tile([128, 512], BF16, "kvn")

        def rstd_from(ms_ps, b_ms, np_, dst=None):
            t, b_t = dst if dst is not None else tf.next()
            k.op("act", lambda e: e.activation(out=t[0:np_, :], in_=ms_ps[0:np_, :], func=AF.Sqrt, bias=epsc[0:np_, :], scale=1.0),
                 r=[b_ms, b_eps], w=[b_t])
            k.op("dve", lambda e: e.reciprocal(out=t[0:np_, :], in_=t[0:np_, :]), r=[b_t], w=[b_t])
            return t, b_t

        def norm_epilogue(ps, b_ps, np_, blk_ap, gain_ap, b_gain, out_dtype_bf=True):
            sq, b_sq = tb.next()
            k.op("act", lambda e: e.activation(out=sq[0:np_, :], in_=ps[0:np_, :], func=AF.Square), r=[b_ps], w=[b_sq])
            ms, b_ms = PS.next()
            k.op("pe", lambda e: e.matmul(ms[0:np_, :], lhsT=blk_ap, rhs=sq[0:np_, :], start=True, stop=True), r=[b_sq, b_cm], w=[b_ms])
            rs, b_rs = rstd_from(ms, b_ms, np_)
            t, b_t = tf.next()
            k.op("dve", lambda e: e.tensor_tensor(out=t[0:np_, :], in0=ps[0:np_, :], in1=rs[0:np_, :], op=ALU.mult), r=[b_ps, b_rs], w=[b_t])
            o, b_o = ob.next()
            k.op("act", lambda e: e.activation(out=o[0:np_, :], in_=t[0:np_, :], func=AF.Identity, scale=gain_ap), r=[b_t, b_gain], w=[b_o])
            return o, b_o

        for ch in range(NCH):
            t0 = ch * 512
            xc, b_xc = xch.next()
            k.dma(xc[:].rearrange("p (i n) -> p i n", i=8), xT[:, t0:t0 + 512].rearrange("(i p) n -> p i n", p=128), w=[b_xc])
            if prev:
                mc, b_mc = mch.next()
                k.dma(mc[:].rearrange("p (i n) -> p i n", i=8), mixT[:, t0:t0 + 512].rearrange("(i p) n -> p i n", p=128), w=[b_mc])
                for j in range(8):
                    yp, b_yp = PS.next()
                    for i in range(8):
                        k.op("pe", (lambda yp=yp, i=i, j=j, mc=mc: lambda e: e.matmul(
                            yp[:], lhsT=wo_b[:, i * D + j * 128:i * D + (j + 1) * 128], rhs=mc[:, i * 512:(i + 1) * 512],
                            start=(i == 0), stop=(i == 7)))(), r=[b_wo, b_mc], w=[b_yp], chain=True)
                    k.op("dve", (lambda yp=yp, j=j, xc=xc: lambda e: e.scalar_tensor_tensor(
                        out=xc[:, j * 512:(j + 1) * 512], in0=yp[:], scalar=gate[:, j:j + 1], in1=xc[:, j * 512:(j + 1) * 512],
                        op0=ALU.mult, op1=ALU.add))(), r=[b_yp, b_gate, b_xc], w=[b_xc])
                k.dma(xT_new[:, t0:t0 + 512].rearrange("(i p) n -> p i n", p=128), xc[:].rearrange("p (i n) -> p i n", i=8), r=[b_xc], q="pool")
            if not nxt:
                continue
            sq, b_sq = sqx.next()
            for i in range(8):
                k.op("act", (lambda i=i, sq=sq, xc=xc: lambda e: e.activation(out=sq[:, i * 512:(i + 1) * 512], in_=xc[:, i * 512:(i + 1) * 512], func=AF.Square))(),
                     r=[b_xc], w=[b_sq])
            ms, b_ms = PS.next()
            for i in range(8):
                k.op("pe", (lambda i=i, sq=sq, ms=ms: lambda e: e.matmul(ms[:], lhsT=ONES1024, rhs=sq[:, i * 512:(i + 1) * 512], start=(i == 0), stop=(i == 7)))(),
                     r=[b_sq, b_cm], w=[b_ms], chain=True)
            rs, b_rs = rstd_from(ms, b_ms, 128, dst=rsx)
            hc, b_hc = hch.next()
            for i in range(8):
                t, b_t = tf.next()
                k.op("dve", (lambda i=i, t=t, xc=xc, rs=rs: lambda e: e.tensor_tensor(out=t[:], in0=xc[:, i * 512:(i + 1) * 512], in1=rs[:], op=ALU.mult))(),
                     r=[b_xc, b_rs], w=[b_t])
                k.op("act", (lambda i=i, t=t, hc=hc: lambda e: e.activation(out=hc[:, i * 512:(i + 1) * 512], in_=t[:], func=AF.Identity,
                                                                             bias=ss[:, i:i + 1], scale=Acol[:, i:i + 1]))(),
                     r=[b_t, b_ss, b_A], w=[b_hc])

            if debug and ch == 0:
                dbg_h = dout("dbg_h", [128, 4096], BF16)
                dbg_s = dout("dbg_s", [128, 24])
                k.dma(dbg_h[:, :], hc[:], r=[b_hc], q="pool")
                k.dma(dbg_s[:, 0:16], ss[:], r=[b_ss], q="pool")
                k.dma(dbg_s[:, 16:24], Acol[:], r=[b_A], q="pool")

            def proj_fm(c0, ncols):
                pst, b_p = PS.next()
                for i in range(8):
                    k.op("pe", (lambda i=i, pst=pst, hc=hc: lambda e: e.matmul(
                        pst[0:ncols, :], lhsT=w_b[:, i * NIN + c0:i * NIN + c0 + ncols], rhs=hc[:, i * 512:(i + 1) * 512],
                        start=(i == 0), stop=(i == 7)))(), r=[b_w, b_hc], w=[b_p], chain=True)
                return pst, b_p

            def store_fm(dst, row0, nrows, t, b_t):
                k.dma(dst[row0:row0 + nrows, t0:t0 + 512], t[0:nrows, :], r=[b_t], q="pool")

            def job_norm(c0, gcol, row0):
                pst, b_p = proj_fm(c0, 128)
                o, b_o = norm_epilogue(pst, b_p, 128, BLK64, gsc[:, gcol:gcol + 1], b_gsc)
                store_fm(fmb, row0, 128, o, b_o)

            def job_act(c0, ncols, func, row0):
                pst, b_p = proj_fm(c0, ncols)
                o, b_o = of.next()
                k.op("act", lambda e: e.activation(out=o[0:ncols, :], in_=pst[0:ncols, :], func=func), r=[b_p], w=[b_o])
                store_fm(fmf, row0, ncols, o, b_o)

            def job_rawb(c0, ncols, row0):
                pst, b_p = proj_fm(c0, ncols)
                o, b_o = ob.next()
                k.op("dve", lambda e: e.tensor_copy(out=o[0:ncols, :], in_=pst[0:ncols, :]), r=[b_p], w=[b_o])
                store_fm(fmb, row0, ncols, o, b_o)

            def job_tm(c0, ncols, col0):
                for tt in range(4):
                    pst, b_p = PS.next()
                    for i in range(8):
                        k.op("pe", (lambda i=i, pst=pst, hc=hc, tt=tt: lambda e: e.matmul(
                            pst[:, 0:ncols], lhsT=hc[:, i * 512 + tt * 128:i * 512 + (tt + 1) * 128],
                            rhs=w_b[:, i * NIN + c0:i * NIN + c0 + ncols], start=(i == 0), stop=(i == 7)))(),
                            r=[b_w, b_hc], w=[b_p], chain=True)
                    o, b_o = ob.next()
                    if tt % 2:
                        k.op("dve", (lambda o=o, pst=pst: lambda e: e.tensor_copy(out=o[:, 0:ncols], in_=pst[:, 0:ncols]))(), r=[b_p], w=[b_o])
                    else:
                        k.op("act", (lambda o=o, pst=pst: lambda e: e.activation(out=o[:, 0:ncols], in_=pst[:, 0:ncols], func=AF.Identity))(), r=[b_p], w=[b_o])
                    k.dma(tmb[t0 + tt * 128:t0 + (tt + 1) * 128, col0:col0 + ncols], o[:, 0:ncols], r=[b_o], q="pool")

            if nxt == "odd":
                for j in range(6):
                    job_norm(j * 128, 0, j * 128)
                for j in range(3):
                    job_norm(768 + j * 128, 1, 768 + j * 128)
                for j in range(6):
                    job_norm(1152 + j * 128, 2, 1152 + j * 128)
                for j in range(6):
                    job_norm(1920 + j * 128, 3, 1920 + j * 128)
                for j in range(3):
                    job_rawb(2688 + j * 128, 128, 2688 + j * 128)
                for j in range(8):
                    job_act(3072 + j * 128, 128, AF.Silu, j * 128)
                job_act(4096, 36, AF.Sigmoid, 1024)
                job_tm(4132, 512, 0)
                job_tm(4132 + 512, 512, 512)
                job_tm(4132 + 1024, 128, 1024)
            else:
                pos_i, b_posi = rope_t["pos_i"]
                k.dma(pos_i[:], pos[t0:t0 + 512].partition_broadcast(96), w=[b_posi])
                ang, b_ang = rope_t["ang"]
                k.op("dve", lambda e: e.tensor_copy(out=ang[:], in_=pos_i[:]), r=[b_posi], w=[b_ang])
                k.op("dve", lambda e: e.tensor_scalar(out=ang[:], in0=ang[:], scalar1=invft[:, 0:1], scalar2=None, op0=ALU.mult), r=[b_ang, b_invf], w=[b_ang])
                tabs = []
                for which in range(2):
                    a2, b_a2 = rope_t["a2_%d" % which]
                    kf, b_kf = rope_t["kf"]
                    ki, b_ki = rope_t["ki"]
                    sh = 0.0 if which == 0 else math.pi / 2
                    k.op("dve", (lambda a2=a2, sh=sh: lambda e: e.tensor_scalar(out=a2[:], in0=ang[:], scalar1=sh, scalar2=None, op0=ALU.add))(), r=[b_ang], w=[b_a2])
                    k.op("dve", (lambda a2=a2, kf=kf: lambda e: e.tensor_scalar(out=kf[:], in0=a2[:], scalar1=1.0 / TWO_PI, scalar2=None, op0=ALU.mult))(), r=[b_a2], w=[b_kf])
                    k.op("dve", (lambda ki=ki, kf=kf: lambda e: e.tensor_copy(out=ki[:], in_=kf[:]))(), r=[b_kf], w=[b_ki])
                    k.op("dve", (lambda ki=ki, kf=kf: lambda e: e.tensor_copy(out=kf[:], in_=ki[:]))(), r=[b_ki], w=[b_kf])
                    k.op("dve", (lambda a2=a2, kf=kf: lambda e: e.scalar_tensor_tensor(out=a2[:], in0=kf[:], scalar=-TWO_PI, in1=a2[:], op0=ALU.mult, op1=ALU.add))(), r=[b_kf, b_a2], w=[b_a2])
                    k.op("dve", (lambda a2=a2, kf=kf: lambda e: e.tensor_scalar(out=kf[:], in0=a2[:], scalar1=math.pi, scalar2=-TWO_PI, op0=ALU.is_gt, op1=ALU.mult))(), r=[b_a2], w=[b_kf])
                    k.op("dve", (lambda a2=a2, kf=kf: lambda e: e.tensor_tensor(out=a2[:], in0=a2[:], in1=kf[:], op=ALU.add))(), r=[b_a2, b_kf], w=[b_a2])
                    k.op("dve", (lambda a2=a2, kf=kf: lambda e: e.tensor_scalar(out=kf[:], in0=a2[:], scalar1=-math.pi, scalar2=TWO_PI, op0=ALU.is_lt, op1=ALU.mult))(), r=[b_a2], w=[b_kf])
                    k.op("dve", (lambda a2=a2, kf=kf: lambda e: e.tensor_tensor(out=a2[:], in0=a2[:], in1=kf[:], op=ALU.add))(), r=[b_a2, b_kf], w=[b_a2])
                    k.op("act", (lambda a2=a2: lambda e: e.activation(out=a2[:], in_=a2[:], func=AF.Sin))(), r=[b_a2], w=[b_a2])
                    tabs.append((a2, b_a2))
                (sinT, b_sin), (cosT, b_cos) = tabs

                def head96(pst, b_p, gcol, dst_row0, row_lo):
                    o, b_o = norm_epilogue(pst, b_p, 96, BLK96, g96s[:, gcol:gcol + 1], b_g96)
                    rq, b_rq = PS.next()
                    k.op("pe", lambda e: e.matmul(rq[0:96, :], lhsT=rot_b[:], rhs=o[0:96, :], start=True, stop=True), r=[b_rot, b_o], w=[b_rq])
                    t1, b_t1 = tf.next()
                    k.op("pool", lambda e: e.tensor_tensor(out=t1[0:96, :], in0=o[0:96, :], in1=cosT[:], op=ALU.mult), r=[b_o, b_cos], w=[b_t1])
                    t2, b_t2 = tf.next()
                    k.op("dve", lambda e: e.tensor_tensor(out=t2[0:96, :], in0=rq[0:96, :], in1=sinT[:], op=ALU.mult), r=[b_rq, b_sin], w=[b_t2])
                    o2, b_o2 = ob.next()
                    k.op("dve", lambda e: e.tensor_tensor(out=o2[0:96, :], in0=t1[0:96, :], in1=t2[0:96, :], op=ALU.add), r=[b_t1, b_t2], w=[b_o2])
                    k.dma(fmb[dst_row0:dst_row0 + 96 - row_lo, t0:t0 + 512], o2[row_lo:96, :], r=[b_o2], q="pool")

                for j in range(4):
                    job_norm(j * 128, 0, j * 128)
                for j in range(4):
                    job_norm(512 + j * 128, 1, 512 + j * 128)
                job_tm(1024, 512, 0)
                for j in range(4):
                    job_act(1536 + j * 128, 128, AF.Silu, j * 128)
                for j in range(4):
                    job_act(2464 + j * 128, 128, AF.Silu, 512 + j * 128)
                qls = []
                sqs = []
                for cidx in range(2):
                    pst, b_p = proj_fm(2048 + cidx * 128, 128)
                    sq2, b_sq2 = tb.next()
                    k.op("act", (lambda sq2=sq2, pst=pst: lambda e: e.activation(out=sq2[:], in_=pst[:], func=AF.Square))(), r=[b_p], w=[b_sq2])
                    qls.append((pst, b_p))
                    sqs.append((sq2, b_sq2))
                ms2, b_ms2 = PS.next()
                for cidx in range(2):
                    k.op("pe", (lambda cidx=cidx: lambda e: e.matmul(ms2[:], lhsT=ONES256, rhs=sqs[cidx][0][:], start=(cidx == 0), stop=(cidx == 1)))(),
                         r=[sqs[cidx][1], b_cm], w=[b_ms2], chain=True)
                rs2, b_rs2 = rstd_from(ms2, b_ms2, 128)
                qln = []
                for cidx in range(2):
                    t, b_t = tf.next()
                    k.op("dve", (lambda t=t, cidx=cidx: lambda e: e.tensor_tensor(out=t[:], in0=qls[cidx][0][:], in1=rs2[:], op=ALU.mult))(),
                         r=[qls[cidx][1], b_rs2], w=[b_t])
                    qn_, b_qn = rope_t["qln%d" % cidx]
                    k.op("act", (lambda t=t, qn_=qn_, cidx=cidx: lambda e: e.activation(out=qn_[:], in_=t[:], func=AF.Identity, scale=qagt[:, cidx:cidx + 1]))(),
                         r=[b_t, b_qag], w=[b_qn])
                    qln.append((qn_, b_qn))
                for h in range(8):
                    pst, b_p = PS.next()
                    for cidx in range(2):
                        k.op("pe", (lambda pst=pst, cidx=cidx, h=h: lambda e: e.matmul(
                            pst[0:96, :], lhsT=wq_b[:, cidx * 768 + h * 96:cidx * 768 + (h + 1) * 96], rhs=qln[cidx][0][:],
                            start=(cidx == 0), stop=(cidx == 1)))(), r=[b_wq, qln[cidx][1]], w=[b_p], chain=True)
                    head96(pst, b_p, 0, 1024 + h * 96, 0)
                pst, b_p = proj_fm(2304, 128)
                kvn_, b_kvn = norm_epilogue(pst, b_p, 128, ONES128, kvagt[:, 0:1], b_kvag)
                kvn, b_kvn2 = rope_t["kvn"]
                k.op("pool", lambda e: e.tensor_copy(out=kvn[:], in_=kvn_[:]), r=[b_kvn], w=[b_kvn2])
                for j in range(4):
                    pst, b_p = PS.next()
                    k.op("pe", (lambda pst=pst, j=j: lambda e: e.matmul(pst[:], lhsT=wkv_b[:, j * 128:(j + 1) * 128], rhs=kvn[:], start=True, stop=True))(),
                         r=[b_wkv, b_kvn2], w=[b_p])
                    o, b_o = norm_epilogue(pst, b_p, 128, BLK64, gsc[:, 3:4], b_gsc)
                    store_fm(fmb, 1792 + j * 128, 128, o, b_o)
                for tt in range(4):
                    pst, b_p = PS.next()
                    k.op("pe", (lambda pst=pst, tt=tt: lambda e: e.matmul(pst[:], lhsT=kvn[:, tt * 128:(tt + 1) * 128], rhs=wkv_b[:, 512:1024], start=True, stop=True))(),
                         r=[b_wkv, b_kvn2], w=[b_p])
                    o, b_o = ob.next()
                    k.op("dve", (lambda o=o, pst=pst: lambda e: e.tensor_copy(out=o[:], in_=pst[:]))(), r=[b_p], w=[b_o])
                    k.dma(tmb[t0 + tt * 128:t0 + (tt + 1) * 128, 512:1024], o[:], r=[b_o], q="pool")
                pst, b_p = proj_fm(2432 - 64, 96)
                head96(pst, b_p, 1, 2304, 64)
        k.finish()
    return nc


def _cmat():
    m = np.zeros((128, 640), np.float32)
    m[:, 0:128] = 1.0 / 1024
    m[:, 128:256] = _blockdiag([64, 64])
    m[:, 256:384] = 1.0 / 256
    m[:, 384:512] = 1.0 / 128
    m[0:96, 512:608] = _blockdiag([64, 32], 96)
    return m


def _col128(v):
    v = np.asarray(v, np.float32)
    return np.ascontiguousarray(v.reshape(-1, 128).T)


def _dup64(v):
    v = np.asarray(v, np.float32)
    return np.concatenate([v, v])


ODD_PERM = None


def _odd_perm():
    global ODD_PERM
    if ODD_PERM is None:
        o = np.cumsum([0, 768, 192, 192, 192, 192, 192, 192, 36, 768, 768, 768, 768, 256])
        seg = lambda i: np.arange(o[i], o[i + 1])
        ODD_PERM = np.concatenate([seg(0), seg(3), seg(5), seg(9), seg(10), seg(1), seg(2), seg(8), seg(12), seg(7), seg(4), seg(6), seg(11)])
    return ODD_PERM


def _wkv_perm():
    idx = np.arange(1024).reshape(8, 128)
    return np.concatenate([idx[:, :64].reshape(-1), idx[:, 64:].reshape(-1)])


def proj_in_maps(P, layer_prev, layer_next, xT, mixT, Tc, ncores):
    maps = []
    cm = _cmat()
    for c in range(ncores):
        b, hf = c // 2, c % 2
        sl = slice(hf * Tc, (hf + 1) * Tc)
        m = {"xT": np.ascontiguousarray(xT[b][:, sl]), "c_col": _col128(P["c"][b]), "cmat": cm}
        if layer_prev is not None:
            lp = layer_prev
            m["mixT"] = np.ascontiguousarray(mixT[b][:, sl])
            m["w_out"] = np.ascontiguousarray((P["ev_w_out"] if lp % 2 == 0 else P["od_w_out"])[lp // 2])
            m["ada_wg"] = np.ascontiguousarray(P["ada_w"][lp][:, 2 * D:3 * D])
            m["ada_bg"] = _col128(P["ada_b"][lp][2 * D:3 * D])
        if layer_next is not None:
            ln = layer_next
            j = ln // 2
            m["ada_ws"] = np.ascontiguousarray(P["ada_w"][ln][:, 0:2 * D])
            m["ada_bs"] = _col128(P["ada_b"][ln][0:2 * D])
            m["normg"] = _col128(P["norm_g"][ln])
            if ln % 2 == 0:
                m["w_in"] = np.ascontiguousarray(P["ev_w_in"][j])
                m["g128"] = np.ascontiguousarray(np.stack([_dup64(P["sb_qn"][j]), _dup64(P["sb_kn"][j]),
                                                           _dup64(P["mla_kn"][j][:64]), _dup64(P["mla_kn"][j][:64])], axis=1))
                m["wq_up"] = np.ascontiguousarray(P["mla_wq_up"][j])
                m["wkv_up"] = np.ascontiguousarray(P["mla_wkv_up"][j][:, _wkv_perm()])
                m["g96"] = np.ascontiguousarray(np.stack([P["mla_qn"][j], P["mla_kn"][j]], axis=1).astype(np.float32))
                m["qag"] = _col128(P["mla_qa_g"][j])
                m["kvag"] = _col128(P["mla_kva_g"][j])
                m["pos"] = np.ascontiguousarray(P["positions"][sl]).astype(np.int32)
                m["rotm"] = _rot96()
                m["invf"] = _invf96()
            else:
                m["w_in"] = np.ascontiguousarray(P["od_w_in"][j][:, _odd_perm()])
                m["g128"] = np.ascontiguousarray(np.stack([_dup64(P["nsa_qn"][j]), _dup64(P["nsa_kn"][j]),
                                                           _dup64(P["dil_qn"][j]), _dup64(P["dil_kn"][j])], axis=1))
        maps.append(m)
    return maps


def _attn_consts():
    i = np.arange(128)[:, None]
    m = np.arange(896)[None, :]
    tns = np.where(m - i >= 384, 0.0, NEG).astype(np.float32)
    tst = np.where(m - i >= 385, 0.0, NEG).astype(np.float32)
    mst = np.where(m - i >= 385, 1.0, 0.0).astype(np.float32)
    strips = np.concatenate([tns, tst, mst], axis=1)
    j = np.arange(128)[:, None]
    s = np.arange(128)[None, :]
    negui = np.where(j >= s, -1.0, 0.0).astype(np.float32)
    sq = np.concatenate([negui, -np.ones((128, 128), np.float32), np.eye(128, dtype=np.float32)], axis=1)
    return strips, sq


def build_attn_even(S, nsb=4, nmla=4):
    nc = bass.Bass("TRN2", target_bir_lowering=False)
    NT = S // 128
    NCH = S // 512
    nh = nsb + nmla
    din = lambda name, shape, dt=F32: nc.dram_tensor(name, list(shape), dt, kind="ExternalInput").ap()
    sbq = din("sbq", [nsb * 64, S], BF16)
    sbk = din("sbk", [nsb * 64, S], BF16)
    sbv = din("sbv", [128, NT * nsb * 65], BF16)
    sbz = din("sbz", [nsb * 64, S])
    mq = din("mq", [nmla * 96, S], BF16)
    mk = din("mk", [nmla * 96, S], BF16)
    mv = din("mv", [128, NT * nmla * 65], BF16)
    mz = din("mz", [nmla * 64, S])
    strips_d = din("strips_in", [128, 2688])
    sqc_d = din("sqc_in", [128, 384])
    mixT = nc.dram_tensor("mixT", [nh * 64, S], BF16, kind="ExternalOutput").ap()

    with ExitStack() as st:
        k = KB(nc, st)
        stg, b_stg = k.tile([128, 2688], F32, "stg")
        strips, b_str = k.tile([128, 2688], BF16, "strips")
        k.dma(stg[:], strips_d[:, :], w=[b_stg])
        k.op("dve", lambda e: e.tensor_copy(out=strips[:], in_=stg[:]), r=[b_stg], w=[b_str])
        stg2, b_stg2 = k.tile([128, 384], F32, "stg2")
        sqc, b_sqc = k.tile([128, 384], BF16, "sqc")
        k.dma(stg2[:], sqc_d[:, :], w=[b_stg2])
        k.op("dve", lambda e: e.tensor_copy(out=sqc[:], in_=stg2[:]), r=[b_stg2], w=[b_sqc])
        NEGUI = sqc[:, 0:128]
        NEGONES = sqc[:, 128:256]
        IDENT = sqc[:, 256:384]
        ones_f, b_ones = k.tile([128, 64], F32, "ones_f")
        k.op("pool", lambda e: e.memset(ones_f[:], 1.0), w=[b_ones])

        v_sb, b_vsb = k.tile([128, NT * nsb * 65], BF16, "v_sb")
        v_ml, b_vml = k.tile([128, NT * nmla * 65], BF16, "v_ml")
        k.dma(v_sb[:], sbv[:, :], w=[b_vsb])
        k.dma(v_ml[:], mv[:, :], w=[b_vml])

        QK = Rot(k, "qk", [128, 2 * S], BF16, 2)
        PSA = Rot(k, "psa", [128, 512], F32, 2, psum=True)
        PSB = Rot(k, "psb", [128, 512], F32, 2, psum=True)
        PSC = Rot(k, "psc", [128, 512], F32, 2, psum=True)
        PSO = Rot(k, "pso", [128, 512], F32, 1, psum=True)
        PSX = Rot(k, "psx", [128, 512], F32, 1, psum=True)
        e_t = Rot(k, "e_t", [128, 512], F32, 2)
        sp_t = Rot(k, "sp_t", [128, 512], BF16, 2)
        E_t = Rot(k, "E_t", [128, 512], F32, 2)
        w_t = Rot(k, "w_t", [128, 512], BF16, 3)
        z_t = Rot(k, "z_t", [64, 512], F32, 2)
        o_t = Rot(k, "o_t", [64, 512], BF16, 2)
        f_t = Rot(k, "f_t", [128, 512], F32, 2)
        carry, b_carry = k.tile([128, 512], F32, "carry")
        den_sb, b_den = k.tile([128, 512], F32, "den_sb")

        for h in range(nsb):
            qk, b_qk = QK.next()
            k.dma(qk[0:64, 0:S], sbq[h * 64:(h + 1) * 64, :], w=[b_qk])
            k.dma(qk[0:64, S:2 * S], sbk[h * 64:(h + 1) * 64, :], w=[b_qk])
            for c in range(NCH):
                q0 = c * 512
                qs = qk[0:64, q0:q0 + 512]
                zt, b_zt = z_t.next()
                k.dma(zt[:], sbz[h * 64:(h + 1) * 64, q0:q0 + 512], w=[b_zt])
                k.op("pool", lambda e: e.memset(carry[:], 0.0), w=[b_carry])
                O, b_O = PSO.next()
                tiles = list(range(4 * c + 3, -1, -1))
                state = {}

                def stage1(kt):
                    d = kt - 4 * c
                    ks = qk[0:64, S + kt * 128:S + (kt + 1) * 128]
                    A, b_A = PSA.next()
                    k.op("pe", lambda e: e.matmul(A[:], lhsT=ks, rhs=qs, start=True, stop=True), r=[b_qk], w=[b_A])
                    et, b_et = e_t.next()
                    k.op("act", lambda e: e.activation(out=et[:], in_=A[:], func=AF.Exp), r=[b_A], w=[b_et])
                    spt, b_spt = sp_t.next()
                    k.op("act", lambda e: e.activation(out=spt[:], in_=et[:], func=AF.Ln, bias=1.0, scale=1.0), r=[b_et], w=[b_spt])
                    if d >= 0:
                        off = 1792 + 384 - 128 * d
                        k.op("pool", lambda e: e.tensor_tensor(out=spt[:], in0=spt[:], in1=strips[:, off:off + 512], op=ALU.mult), r=[b_spt, b_str], w=[b_spt])
                    Bp, b_B = PSB.next()
                    k.op("pe", lambda e: e.matmul(Bp[:], lhsT=NEGUI, rhs=spt[:], start=True, stop=False), r=[b_sqc, b_spt], w=[b_B])
                    k.op("pe", lambda e: e.matmul(Bp[:], lhsT=ks, rhs=qs, start=False, stop=(d < 0)), r=[b_qk], w=[b_B], chain=True)
                    if d >= 0:
                        off2 = 896 + 384 - 128 * d
                        k.op("pe", lambda e: e.matmul(Bp[:], lhsT=IDENT, rhs=strips[:, off2:off2 + 512], start=False, stop=True), r=[b_sqc, b_str], w=[b_B], chain=True)
                    Cp, b_C = PSC.next()
                    k.op("pe", lambda e: e.matmul(Cp[:], lhsT=NEGONES, rhs=spt[:], start=True, stop=True), r=[b_sqc, b_spt], w=[b_C])
                    state[kt] = (Bp, b_B, Cp, b_C)

                def stage2(kt, first, last):
                    Bp, b_B, Cp, b_C = state.pop(kt)
                    Et, b_Et = E_t.next()
                    k.op("dve", lambda e: e.tensor_tensor(out=Et[:], in0=Bp[:], in1=carry[:], op=ALU.add), r=[b_B, b_carry], w=[b_Et])
                    wt, b_wt = w_t.next()
                    k.op("act", lambda e: e.activation(out=wt[:], in_=Et[:], func=AF.Exp), r=[b_Et], w=[b_wt])
                    if not last:
                        k.op("dve", lambda e: e.tensor_tensor(out=carry[:], in0=Cp[:], in1=carry[:], op=ALU.add), r=[b_C, b_carry], w=[b_carry])
                    vo = (kt * nsb + h) * 65
                    k.op("pe", lambda e: e.matmul(O[0:64, :], lhsT=v_sb[:, vo:vo + 64], rhs=wt[:], start=first, stop=last), r=[b_vsb, b_wt], w=[b_O], chain=True)

                stage1(tiles[0])
                for idx, kt in enumerate(tiles):
                    if idx + 1 < len(tiles):
                        stage1(tiles[idx + 1])
                    stage2(kt, idx == 0, idx == len(tiles) - 1)
                ot, b_ot = o_t.next()
                k.op("dve", lambda e: e.tensor_tensor(out=ot[:], in0=O[0:64, :], in1=zt[:], op=ALU.mult), r=[b_O, b_zt], w=[b_ot])
                k.dma(mixT[h * 64:(h + 1) * 64, q0:q0 + 512], ot[:], r=[b_ot], q="pool")

        for h in range(nmla):
            qk, b_qk = QK.next()
            k.dma(qk[0:96, 0:S], mq[h * 96:(h + 1) * 96, :], w=[b_qk])
            k.dma(qk[0:96, S:2 * S], mk[h * 96:(h + 1) * 96, :], w=[b_qk])
            for c in range(NCH):
                q0 = c * 512
                qs = qk[0:96, q0:q0 + 512]
                zt, b_zt = z_t.next()
                k.dma(zt[:], mz[h * 64:(h + 1) * 64, q0:q0 + 512], w=[b_zt])
                O, b_O = PSO.next()
                tiles = list(range(0, 4 * c + 4))
                state = {}

                def m1(kt):
                    d = kt - 4 * c
                    ks = qk[0:96, S + kt * 128:S + (kt + 1) * 128]
                    A, b_A = PSA.next()
                    k.op("pe", lambda e: e.matmul(A[:], lhsT=ks, rhs=qs, start=True, stop=(d < 0)), r=[b_qk], w=[b_A])
                    if d >= 0:
                        off = 384 - 128 * d
                        k.op("pe", lambda e: e.matmul(A[:], lhsT=IDENT, rhs=strips[:, off:off + 512], start=False, stop=True), r=[b_sqc, b_str], w=[b_A], chain=True)
                    wt, b_wt = w_t.next()
                    k.op("act", lambda e: e.activation(out=wt[:], in_=A[:], func=AF.Exp), r=[b_A], w=[b_wt])
                    state[kt] = (wt, b_wt)

                def m2(kt, first, last):
                    wt, b_wt = state.pop(kt)
                    vo = (kt * nmla + h) * 65
                    k.op("pe", lambda e: e.matmul(O[0:65, :], lhsT=v_ml[:, vo:vo + 65], rhs=wt[:], start=first, stop=last), r=[b_vml, b_wt], w=[b_O], chain=True)

                m1(tiles[0])
                for idx, kt in enumerate(tiles):
                    if idx + 1 < len(tiles):
                        m1(tiles[idx + 1])
                    m2(kt, idx == 0, idx == len(tiles) - 1)
                k.op("act", lambda e: e.activation(out=den_sb[64:65, :], in_=O[64:65, :], func=AF.Identity), r=[b_O], w=[b_den])
                X, b_X = PSX.next()
                k.op("pe", lambda e: e.matmul(X[0:64, :], lhsT=ones_f[64:65, 0:64], rhs=den_sb[64:65, :], start=True, stop=True), r=[b_ones, b_den], w=[b_X])
                ft, b_ft = f_t.next()
                k.op("dve", lambda e: e.reciprocal(out=ft[0:64, :], in_=X[0:64, :]), r=[b_X], w=[b_ft])
                k.op("dve", lambda e: e.tensor_tensor(out=ft[0:64, :], in0=O[0:64, :], in1=ft[0:64, :], op=ALU.mult), r=[b_O, b_ft], w=[b_ft])
                ot, b_ot = o_t.next()
                k.op("pool", lambda e: e.tensor_tensor(out=ot[:], in0=ft[0:64, :], in1=zt[:], op=ALU.mult), r=[b_ft, b_zt], w=[b_ot])
                k.dma(mixT[(nsb + h) * 64:(nsb + h + 1) * 64, q0:q0 + 512], ot[:], r=[b_ot], q="pool")
        k.finish()
    return nc


def _v_layout(v_tm, nheads):
    S = v_tm.shape[0]
    NT = S // 128
    out = np.ones((128, NT, nheads, 65), NPBF)
    out[:, :, :, 0:64] = v_tm.reshape(NT, 128, nheads, 64).transpose(1, 0, 2, 3)
    return np.ascontiguousarray(out.reshape(128, NT * nheads * 65))


DIL_CFG = ((128, 1), (512, 4), (2048, 16))


def _alibi_slopes(n):
    return (2.0 ** (-8.0 * np.arange(1, n + 1, dtype=np.float32) / n)).astype(np.float32)


def _dil_strip(W, dl):
    width = 896 + W
    i = np.arange(128)[:, None]
    m = np.arange(width)[None, :]
    u = m - i - 384
    ok = (u >= 0) & (u <= W) & (u % dl == 0)
    return np.where(ok, 0.0, NEG).astype(np.float32)


class AlibiCtx:
    def __init__(self, k, S, pos_d, poskcol_d):
        self.k = k
        NT = S // 128
        self.posq_i = Rot(k, "posq_i", [128, 512], I32, 2)
        self.negposq = Rot(k, "negposq", [128, 512], F32, 2)
        self.pos_d = pos_d
        pk_i, b_pki = k.tile([128, NT], I32, "pk_i")
        k.dma(pk_i[:], poskcol_d[:, :], w=[b_pki])
        self.posk, self.b_posk = k.tile([128, NT], F32, "posk_f")
        k.op("dve", lambda e: e.tensor_copy(out=self.posk[:], in_=pk_i[:]), r=[b_pki], w=[self.b_posk])
        self.sb1 = Rot(k, "sb1", [128, 512], F32, 3)
        self.P = Rot(k, "Pt", [128, 512], BF16, 3)

    def chunk(self, q0):
        k = self.k
        pi, b_pi = self.posq_i.next()
        k.dma(pi[:], self.pos_d[q0:q0 + 512].partition_broadcast(128), w=[b_pi])
        npq, b_npq = self.negposq.next()
        k.op("dve", lambda e: e.tensor_scalar(out=npq[:], in0=pi[:], scalar1=-1.0, scalar2=None, op0=ALU.mult), r=[b_pi], w=[b_npq])
        self.npq, self.b_npq = npq, b_npq

    def score_to_p(self, A, b_A, slope_ap, b_slope, bias_ap, b_bias, np_=128):
        k = self.k
        s1, b_s1 = self.sb1.next()
        k.op("dve", lambda e: e.scalar_tensor_tensor(out=s1[0:np_, :], in0=self.npq[0:np_, :], scalar=slope_ap, in1=A[0:np_, :],
                                                     op0=ALU.mult, op1=ALU.add), r=[self.b_npq, b_slope, b_A], w=[b_s1])
        P, b_P = self.P.next()
        k.op("act", lambda e: e.activation(out=P[0:np_, :], in_=s1[0:np_, :], func=AF.Exp, bias=bias_ap, scale=1.0), r=[b_s1, b_bias], w=[b_P])
        return P, b_P


def norm_gate_store(k, O, b_O, zt, b_zt, ones_f, b_ones, den_sb, b_den, PSX, f_t, o_t, dst_ap):
    k.op("act", lambda e: e.activation(out=den_sb[64:65, :], in_=O[64:65, :], func=AF.Identity), r=[b_O], w=[b_den])
    X, b_X = PSX.next()
    k.op("pe", lambda e: e.matmul(X[0:64, :], lhsT=ones_f[64:65, 0:64], rhs=den_sb[64:65, :], start=True, stop=True), r=[b_ones, b_den], w=[b_X])
    ft, b_ft = f_t.next()
    k.op("dve", lambda e: e.reciprocal(out=ft[0:64, :], in_=X[0:64, :]), r=[b_X], w=[b_ft])
    k.op("dve", lambda e: e.tensor_tensor(out=ft[0:64, :], in0=O[0:64, :], in1=ft[0:64, :], op=ALU.mult), r=[b_O, b_ft], w=[b_ft])
    ot, b_ot = o_t.next()
    k.op("pool", lambda e: e.tensor_tensor(out=ot[:], in0=ft[0:64, :], in1=zt[:], op=ALU.mult), r=[b_ft, b_zt], w=[b_ot])
    k.dma(dst_ap, ot[:], r=[b_ot], q="pool")


def build_attn_dil(S, nhead=2):
    nc = bass.Bass("TRN2", target_bir_lowering=False)
    NT = S // 128
    NCH = S // 512
    din = lambda name, shape, dt=F32: nc.dram_tensor(name, list(shape), dt, kind="ExternalInput").ap()
    dq = din("dq", [3 * nhead * 64, S], BF16)
    dk = din("dk", [3 * nhead * 64, S], BF16)
    dv = din("dv", [128, 3 * nhead * NT * 65], BF16)
    dz = din("dz", [nhead * 64, S])
    slopes_d = din("slopes", [128, 3 * nhead])
    pos_d = din("pos", [S], I32)
    poskcol_d = din("poskcol", [128, NT], I32)
    sw = [896 + W for W, _ in DIL_CFG]
    so = [0, sw[0], sw[0] + sw[1]]
    strips_d = din("dstrips", [128, sum(sw)], BF16)
    ident_d = din("ident_in", [128, 128], BF16)
    mixT = nc.dram_tensor("mixT", [nhead * 64, S], BF16, kind="ExternalOutput").ap()
    with ExitStack() as st:
        k = KB(nc, st)
        strips, b_str = k.tile([128, sum(sw)], BF16, "strips")
        k.dma(strips[:], strips_d[:, :], w=[b_str])
        ident, b_id = k.tile([128, 128], BF16, "ident")
        k.dma(ident[:], ident_d[:, :], w=[b_id])
        ones_f, b_ones = k.tile([128, 64], F32, "ones_f")
        k.op("pool", lambda e: e.memset(ones_f[:], 1.0), w=[b_ones])
        sl, b_sl = k.tile([128, 3 * nhead], F32, "sl")
        k.dma(sl[:], slopes_d[:, :], w=[b_sl])
        ctx = AlibiCtx(k, S, pos_d, poskcol_d)
        pks, b_pks = k.tile([128, 3 * nhead * NT], F32, "pks")
        for gh in range(3 * nhead):
            k.op("dve", lambda e: e.tensor_scalar(out=pks[:, gh * NT:(gh + 1) * NT], in0=ctx.posk[:], scalar1=sl[:, gh:gh + 1], scalar2=None, op0=ALU.mult),
                 r=[ctx.b_posk, b_sl], w=[b_pks])
        v_l, b_v = k.tile([128, 3 * nhead * NT * 65], BF16, "v_l")
        k.dma(v_l[:], dv[:, :], w=[b_v])
        QT = [k.tile([64 * nhead, S], BF16, "qT%d" % g) for g in range(3)]
        KT = [k.tile([64 * nhead, S], BF16, "kT%d" % g) for g in range(3)]
        for g in range(3):
            k.dma(QT[g][0][:], dq[g * nhead * 64:(g + 1) * nhead * 64, :], w=[QT[g][1]])
            k.dma(KT[g][0][:], dk[g * nhead * 64:(g + 1) * nhead * 64, :], w=[KT[g][1]])
        PSA = Rot(k, "psa", [128, 512], F32, 3, psum=True)
        PSO = Rot(k, "pso", [128, 512], F32, 2, psum=True)
        PSX = Rot(k, "psx", [128, 512], F32, 1, psum=True)
        z_t = Rot(k, "z_t", [64, 512], F32, 2)
        o_t = Rot(k, "o_t", [64, 512], BF16, 2)
        f_t = Rot(k, "f_t", [128, 512], F32, 2)
        den_sb, b_den = k.tile([128, 512], F32, "den_sb")
        for c in range(NCH):
            q0 = c * 512
            ctx.chunk(q0)
            for h in range(nhead):
                zt, b_zt = z_t.next()
                k.dma(zt[:], dz[h * 64:(h + 1) * 64, q0:q0 + 512], w=[b_zt])
                O, b_O = PSO.next()
                work = []
                for g, (W, dl) in enumerate(DIL_CFG):
                    for kt in range(max(0, 4 * c - W // 128), 4 * c + 4):
                        work.append((g, kt))
                pend = []
                for idx, (g, kt) in enumerate(work):
                    d = kt - 4 * c
                    gh = g * nhead + h
                    qs = QT[g][0][h * 64:(h + 1) * 64, q0:q0 + 512]
                    ks = KT[g][0][h * 64:(h + 1) * 64, kt * 128:(kt + 1) * 128]
                    A, b_A = PSA.next()
                    k.op("pe", lambda e: e.matmul(A[:], lhsT=ks, rhs=qs, start=True, stop=False), r=[QT[g][1], KT[g][1]], w=[b_A])
                    off = so[g] + 384 - 128 * d
                    k.op("pe", lambda e: e.matmul(A[:], lhsT=ident[:], rhs=strips[:, off:off + 512], start=False, stop=True), r=[b_id, b_str], w=[b_A], chain=True)
                    P, b_P = ctx.score_to_p(A, b_A, sl[:, gh:gh + 1], b_sl, pks[:, gh * NT + kt:gh * NT + kt + 1], b_pks)
                    pend.append((P, b_P, gh, kt, idx))
                    if len(pend) > 1:
                        P2, b_P2, gh2, kt2, idx2 = pend.pop(0)
                        vo = (gh2 * NT + kt2) * 65
                        k.op("pe", lambda e: e.matmul(O[0:65, :], lhsT=v_l[:, vo:vo + 65], rhs=P2[:], start=(idx2 == 0), stop=False), r=[b_v, b_P2], w=[b_O], chain=True)
                P2, b_P2, gh2, kt2, idx2 = pend.pop(0)
                vo = (gh2 * NT + kt2) * 65
                k.op("pe", lambda e: e.matmul(O[0:65, :], lhsT=v_l[:, vo:vo + 65], rhs=P2[:], start=(idx2 == 0), stop=True), r=[b_v, b_P2], w=[b_O], chain=True)
                norm_gate_store(k, O, b_O, zt, b_zt, ones_f, b_ones, den_sb, b_den, PSX, f_t, o_t, mixT[h * 64:(h + 1) * 64, q0:q0 + 512])
        k.finish()
    return nc


def _v_layout_multi(v_list):
    S = v_list[0].shape[0]
    NT = S // 128
    out = np.ones((128, len(v_list), NT, 65), NPBF)
    for i, v in enumerate(v_list):
        out[:, i, :, 0:64] = v.reshape(NT, 128, 64).transpose(1, 0, 2)
    return np.ascontiguousarray(out.reshape(128, -1))


def _poskcol(positions, S):
    return np.ascontiguousarray(np.asarray(positions[:S], np.int32).reshape(S // 128, 128).T)


def _nsa_consts(S):
    NJ = S // 64
    ncmp = S // 16 - 1
    NCT = max(1, S // 2048)
    NCH = S // 512
    q = np.arange(S)[:, None]
    j = np.arange(NJ)[None, :]
    cur = q // 64
    forced = (j == 0) | (j == cur) | (j == cur - 1)
    fm = np.where(j <= cur, 1000.0 * forced, -1e30).astype(np.float32)
    n = np.arange(NCT * 128)[:, None]
    ovl = ((16 * n <= 64 * j + 63) & (16 * n + 31 >= 64 * j) & (n < ncmp)).astype(np.float32)
    ovl_aug = np.concatenate([ovl, np.ones((NCT * 128, 1), np.float32)], axis=1)
    ovl_l = np.ascontiguousarray(ovl_aug.reshape(NCT, 128, NJ + 1).transpose(1, 0, 2).reshape(128, NCT * (NJ + 1)))
    cthr = np.zeros((128, NCT * NCH), np.float32)
    for t in range(NCT):
        for c in range(NCH):
            nn = t * 128 + np.arange(128)
            cthr[:, t * NCH + c] = np.where(nn < ncmp, 16.0 * nn + 31 - 512 * c, 1e9)
    iota = np.broadcast_to(np.arange(512, dtype=np.float32)[None, :], (128, 512)).copy()
    efull = np.zeros((128, S), np.float32)
    kk = np.arange(S)
    efull[kk // 64 % 128, kk] = 1.0 if NJ <= 128 else 0.0
    i = np.arange(128)[:, None]
    m = np.arange(1408)[None, :]
    wst = np.where((m - i >= 384) & (m - i <= 895), 0.0, NEG).astype(np.float32)
    m2 = np.arange(896)[None, :]
    tns = np.where(m2 - i >= 384, 0.0, NEG).astype(np.float32)
    return dict(fm=fm, ovl=ovl_l.astype(NPBF), cthr=cthr, iota=iota, efull=efull.astype(NPBF), wst=wst.astype(NPBF), tns=tns.astype(NPBF))


def build_attn_nsa(S, U=3):
    nc = bass.Bass("TRN2", target_bir_lowering=False)
    NT = S // 128
    NCH = S // 512
    NJ = S // 64
    ncmp = S // 16 - 1
    NCT = max(1, S // 2048)
    din = lambda name, shape, dt=F32: nc.dram_tensor(name, list(shape), dt, kind="ExternalInput").ap()
    nq_d = din("nq", [U, 256, S], BF16)
    ksel_d = din("ksel", [U, 128, S], BF16)
    kwin_d = din("kwin", [U, 128, S], BF16)
    ckcv_d = din("ckcv", [U, 128, S], BF16)
    vsel_d = din("vsel", [U, 128, NT * 65], BF16)
    vwin_d = din("vwin", [U, 128, NT * 65], BF16)
    gates_d = din("gates", [U, 6, S])
    nz_d = din("nz", [U, 128, S])
    slopes_d = din("slopes", [U, 128, 4])
    wcmp_d = din("wcmp", [128, 32 * 64])
    peT_d = din("peT", [128, 32])
    kng_d = din("kng", [64, 1])
    pos_d = din("pos", [S], I32)
    poskcol_d = din("poskcol", [128, NT], I32)
    fm_d = din("fm", [S, NJ])
    ovl_d = din("ovl", [128, NCT * (NJ + 1)], BF16)
    cthr_d = din("cthr", [128, NCT * NCH])
    iota_d = din("iota", [128, 512])
    efull_d = din("efull", [128, S], BF16)
    wst_d = din("wst", [128, 1408], BF16)
    tns_d = din("tns", [128, 896], BF16)
    ident_d = din("ident_in", [128, 128], BF16)
    mixT = nc.dram_tensor("mixT", [U, 128, S], BF16, kind="ExternalOutput").ap()

    with ExitStack() as st:
        k = KB(nc, st)

        def load_cast(src, shape, name):
            f, b_f = k.tile(shape, F32, name + "_f")
            b, b_b = k.tile(shape, BF16, name)
            k.dma(f[:], src, w=[b_f])
            k.op("dve", lambda e: e.tensor_copy(out=b[:], in_=f[:]), r=[b_f], w=[b_b])
            return b, b_b

        def load_b(src, shape, name):
            b, b_b = k.tile(shape, BF16, name)
            k.dma(b[:], src, w=[b_b])
            return b, b_b

        wst, b_wst = load_b(wst_d[:, :], [128, 1408], "wst")
        tns, b_tns = load_b(tns_d[:, :], [128, 896], "tns")
        ident, b_id = load_b(ident_d[:, :], [128, 128], "ident")
        ovl, b_ovl = load_b(ovl_d[:, :], [128, NCT * (NJ + 1)], "ovl")
        wc_b, b_wc = load_cast(wcmp_d[:, :], [128, 2048], "wc_b")
        pe_b, b_pe = load_cast(peT_d[:, :], [128, 32], "pe_b")
        efull, b_ef = k.tile([128, S], BF16, "efull")
        k.dma(efull[:], efull_d[:, :], w=[b_ef])
        cthr, b_cthr = k.tile([128, NCT * NCH], F32, "cthr")
        k.dma(cthr[:], cthr_d[:, :], w=[b_cthr])
        iota, b_iota = k.tile([128, 512], F32, "iota")
        k.dma(iota[:], iota_d[:, :], w=[b_iota])
        kng, b_kng = k.tile([64, 1], F32, "kng")
        k.dma(kng[:], kng_d[:, :], w=[b_kng])
        ones_f, b_ones = k.tile([128, 128], F32, "ones_f")
        k.op("pool", lambda e: e.memset(ones_f[:], 1.0), w=[b_ones])
        ones64, b_o64 = k.tile([64, 64], BF16, "ones64")
        k.op("pool", lambda e: e.memset(ones64[:], 1.0 / 64), w=[b_o64])
        epsc, b_eps = k.tile([128, 1], F32, "epsc")
        k.op("pool", lambda e: e.memset(epsc[:], EPS), w=[b_eps])

        ctx = AlibiCtx(k, S, pos_d, poskcol_d)
        PSA = Rot(k, "psa", [128, 512], F32, 2, psum=True)
        PSO = Rot(k, "pso", [128, 512], F32, 2, psum=True)
        IMP = [k.ps("imp%d" % i) for i in range(2)]
        b_IMP = [k.buf() for i in range(2)]
        PSX = Rot(k, "psx", [128, 512], F32, 1, psum=True)
        PST = Rot(k, "pst", [128, 512], F32, 1, psum=True)

        nrow_last = ncmp - 128 * (NCT - 1)
        pa_i, b_pai = k.tile([128, NCT * 16], I32, "pa_i")
        pb_i, b_pbi = k.tile([128, NCT * 16], I32, "pb_i")
        k.op("pool", lambda e: e.memset(pb_i[:], 0), w=[b_pbi])
        k.dma(pa_i[:].rearrange("p (t l) -> p t l", l=16), pos_d[0:NCT * 128 * 16].rearrange("(t p l) -> p t l", p=128, l=16), w=[b_pai])
        if NCT > 1:
            k.dma(pb_i[:, 0:(NCT - 1) * 16].rearrange("p (t l) -> p t l", l=16),
                  pos_d[16:16 + (NCT - 1) * 128 * 16].rearrange("(t p l) -> p t l", p=128, l=16), w=[b_pbi])
        o_last = 16 + (NCT - 1) * 128 * 16
        k.dma(pb_i[0:nrow_last, (NCT - 1) * 16:NCT * 16], pos_d[o_last:o_last + nrow_last * 16].rearrange("(p l) -> p l", l=16), w=[b_pbi])
        pab, b_pab = k.tile([128, NCT * 32], F32, "pab")
        k.op("dve", lambda e: e.tensor_copy(out=pab[:, 0:NCT * 16], in_=pa_i[:]), r=[b_pai], w=[b_pab])
        k.op("dve", lambda e: e.tensor_copy(out=pab[:, NCT * 16:NCT * 32], in_=pb_i[:]), r=[b_pbi], w=[b_pab])
        cpos, b_cpos = k.tile([128, NCT], F32, "cpos")
        csum, b_csum = k.tile([128, 2 * NCT], F32, "csum")
        k.op("dve", lambda e: e.tensor_reduce(out=csum[:], in_=pab[:].rearrange("p (t l) -> p t l", l=16), axis=mybir.AxisListType.X, op=ALU.add), r=[b_pab], w=[b_csum])
        k.op("dve", lambda e: e.tensor_tensor(out=cpos[:], in0=csum[:, 0:NCT], in1=csum[:, NCT:2 * NCT], op=ALU.add), r=[b_csum], w=[b_cpos])
        k.op("dve", lambda e: e.tensor_scalar(out=cpos[:], in0=cpos[:], scalar1=1.0 / 32, scalar2=None, op0=ALU.mult), r=[b_cpos], w=[b_cpos])

        X, b_X = PSX.next()
        for l in range(32):
            k.op("pe", lambda e: e.matmul(X[0:64, 0:1], lhsT=wc_b[0:64, l * 64:(l + 1) * 64], rhs=pe_b[0:64, l:l + 1], start=(l == 0), stop=(l == 31)),
                 r=[b_wc, b_pe], w=[b_X], chain=True)
        constk, b_ck = k.tile([64, 1], F32, "constk")
        k.op("act", lambda e: e.activation(out=constk[:], in_=X[0:64, 0:1], func=AF.Identity), r=[b_X], w=[b_ck])
        X2, b_X2 = PST.next()
        for l in range(32):
            k.op("pe", lambda e: e.matmul(X2[0:1, 0:64], lhsT=pe_b[64:128, l:l + 1], rhs=wc_b[64:128, l * 64:(l + 1) * 64], start=(l == 0), stop=(l == 31)),
                 r=[b_wc, b_pe], w=[b_X2], chain=True)
        constv, b_cv = k.tile([1, 64], F32, "constv")
        k.op("act", lambda e: e.activation(out=constv[:], in_=X2[0:1, 0:64], func=AF.Identity), r=[b_X2], w=[b_cv])

        q01, b_q01 = k.tile([128, S], BF16, "q01")
        q23, b_q23 = k.tile([128, S], BF16, "q23")
        ksel, b_ksel = k.tile([128, S], BF16, "ksel")
        kwin, b_kwin = k.tile([128, S], BF16, "kwin")
        ckcv, b_ckcv = k.tile([128, S], BF16, "ckcv")
        vsel, b_vsel = k.tile([128, NT * 65], BF16, "vsel")
        vwin, b_vwin = k.tile([128, NT * 65], BF16, "vwin")
        kc_f, b_kcf = k.tile([64, 512], F32, "kc_f")
        kcT2, b_kc = k.tile([128, 512], BF16, "kcT2")
        vc_l, b_vc = k.tile([128, NCT * 65], BF16, "vc_l")
        sl, b_sl = k.tile([128, 4], F32, "sl")
        pks, b_pks = k.tile([128, 2 * NT], F32, "pks")
        cposs, b_cposs = k.tile([128, 4 * NCT], F32, "cposs")
        fm_t = Rot(k, "fm_t", [128, 4 * NJ], F32, 2)
        g_t = Rot(k, "g_t", [128, 6 * 512], F32, 1)
        z_t = Rot(k, "z_t", [64, 512], F32, 2)
        o_t = Rot(k, "o_t", [64, 512], BF16, 2)
        f_t = Rot(k, "f_t", [64, 512], F32, 2)
        cm_t = Rot(k, "cm_t", [128, 512], BF16, 2)
        accO = [k.tile([64, 512], F32, "accO%d" % i) for i in range(2)]
        acc_imp, b_acc = k.tile([128, 4 * NJ], F32, "acc_imp")
        rden, b_rden = k.tile([128, 4], F32, "rden")
        sc_t = Rot(k, "sc_t", [128, NJ], F32, 2)
        wk_t = Rot(k, "wk_t", [128, NJ], F32, 2)
        m8, b_m8 = k.tile([128, 16], F32, "m8")
        sn_t = Rot(k, "sn_t", [128, NJ], BF16, 2)
        selnegT, b_snT = k.tile([128, 512], BF16, "selnegT")
        den_sb, b_den = k.tile([128, 512], F32, "den_sb")
        sq_t, b_sqt = k.tile([64, 512], BF16, "sq_t")
        rs_t, b_rst = k.tile([64, 512], F32, "rs_t")

        def combine_branch(O, b_O, i, br, gt, b_gt):
            k.op("act", lambda e: e.activation(out=den_sb[64:65, :], in_=O[64:65, :], func=AF.Identity), r=[b_O], w=[b_den])
            k.op("dve", lambda e: e.tensor_scalar(out=den_sb[64:65, :], in0=den_sb[64:65, :], scalar1=1e-30, scalar2=None, op0=ALU.max), r=[b_den], w=[b_den])
            k.op("dve", lambda e: e.reciprocal(out=den_sb[64:65, :], in_=den_sb[64:65, :]), r=[b_den], w=[b_den])
            go = (i * 3 + br) * 512
            k.op("dve", lambda e: e.tensor_tensor(out=den_sb[64:65, :], in0=den_sb[64:65, :], in1=gt[64:65, go:go + 512], op=ALU.mult), r=[b_den, b_gt], w=[b_den])
            Xb, b_Xb = PSX.next()
            k.op("pe", lambda e: e.matmul(Xb[0:64, :], lhsT=ones_f[64:65, 0:64], rhs=den_sb[64:65, :], start=True, stop=True), r=[b_ones, b_den], w=[b_Xb])
            ft, b_ft = f_t.next()
            k.op("act", lambda e: e.activation(out=ft[:], in_=Xb[0:64, :], func=AF.Identity), r=[b_Xb], w=[b_ft])
            a, b_a = accO[i]
            if br == 0:
                k.op("dve", lambda e: e.tensor_tensor(out=a[:], in0=O[0:64, :], in1=ft[:], op=ALU.mult), r=[b_O, b_ft], w=[b_a])
            else:
                k.op("dve", lambda e: e.tensor_tensor(out=ft[:], in0=O[0:64, :], in1=ft[:], op=ALU.mult), r=[b_O, b_ft], w=[b_ft])
                k.op("pool", lambda e: e.tensor_tensor(out=a[:], in0=a[:], in1=ft[:], op=ALU.add), r=[b_a, b_ft], w=[b_a])

        for u in range(U):
            k.dma(q01[:], nq_d[u, 0:128, :], w=[b_q01])
            k.dma(q23[:], nq_d[u, 128:256, :], w=[b_q23])
            k.dma(ksel[:], ksel_d[u, :, :], w=[b_ksel])
            k.dma(kwin[:], kwin_d[u, :, :], w=[b_kwin])
            k.dma(ckcv[:], ckcv_d[u, :, :], w=[b_ckcv])
            k.dma(vsel[:], vsel_d[u, :, :], w=[b_vsel])
            k.dma(vwin[:], vwin_d[u, :, :], w=[b_vwin])
            k.dma(sl[:], slopes_d[u, :, :], w=[b_sl])
            for i in range(2):
                k.op("dve", lambda e: e.tensor_scalar(out=pks[:, i * NT:(i + 1) * NT], in0=ctx.posk[:], scalar1=sl[:, i:i + 1], scalar2=None, op0=ALU.mult),
                     r=[ctx.b_posk, b_sl], w=[b_pks])
            for hh in range(4):
                k.op("dve", lambda e: e.tensor_scalar(out=cposs[:, hh * NCT:(hh + 1) * NCT], in0=cpos[:], scalar1=sl[:, hh:hh + 1], scalar2=None, op0=ALU.mult),
                     r=[b_cpos, b_sl], w=[b_cposs])
            A, b_A = PSA.next()
            for l in range(32):
                k.op("pe", lambda e: e.matmul(A[0:64, 0:ncmp], lhsT=wc_b[0:64, l * 64:(l + 1) * 64], rhs=ckcv[0:64, l:l + 16 * (ncmp - 1) + 1:16],
                                              start=(l == 0), stop=(l == 31)), r=[b_wc, b_ckcv], w=[b_A], chain=True)
            k.op("pool", lambda e: e.memset(kc_f[:], 0.0), w=[b_kcf])
            k.op("act", lambda e: e.activation(out=kc_f[:, 0:ncmp], in_=A[0:64, 0:ncmp], func=AF.Identity, bias=constk[:, 0:1], scale=1.0), r=[b_A, b_ck], w=[b_kcf])
            k.op("act", lambda e: e.activation(out=sq_t[:], in_=kc_f[:], func=AF.Square), r=[b_kcf], w=[b_sqt])
            X, b_X = PSX.next()
            k.op("pe", lambda e: e.matmul(X[0:64, :], lhsT=ones64[:], rhs=sq_t[:], start=True, stop=True), r=[b_o64, b_sqt], w=[b_X])
            k.op("act", lambda e: e.activation(out=rs_t[:], in_=X[0:64, :], func=AF.Sqrt, bias=epsc[0:64, :], scale=1.0), r=[b_X, b_eps], w=[b_rst])
            k.op("dve", lambda e: e.reciprocal(out=rs_t[:], in_=rs_t[:]), r=[b_rst], w=[b_rst])
            k.op("dve", lambda e: e.tensor_tensor(out=kc_f[:], in0=kc_f[:], in1=rs_t[:], op=ALU.mult), r=[b_kcf, b_rst], w=[b_kcf])
            k.op("act", lambda e: e.activation(out=kcT2[0:64, :], in_=kc_f[:], func=AF.Identity, scale=kng[:, 0:1]), r=[b_kcf, b_kng], w=[b_kc])
            k.dma(kcT2[64:128, :], kcT2[0:64, :], r=[b_kc], w=[b_kc])
            k.op("pool", lambda e: e.memset(vc_l[:], 0.0), w=[b_vc])
            for t in range(NCT):
                k.op("pool", lambda e: e.memset(vc_l[:, t * 65 + 64:t * 65 + 65], 1.0), w=[b_vc])
            for t in range(NCT):
                nr = min(128, ncmp - 128 * t)
                Xv, b_Xv = PST.next()
                for l in range(32):
                    c0 = 16 * 128 * t + l
                    k.op("pe", lambda e: e.matmul(Xv[0:nr, 0:64], lhsT=ckcv[64:128, c0:c0 + 16 * (nr - 1) + 1:16], rhs=wc_b[64:128, l * 64:(l + 1) * 64],
                                                  start=(l == 0), stop=False), r=[b_wc, b_ckcv], w=[b_Xv], chain=True)
                k.op("pe", lambda e: e.matmul(Xv[0:nr, 0:64], lhsT=ones_f[0:1, 0:nr], rhs=constv[0:1, 0:64], start=False, stop=True), r=[b_ones, b_cv], w=[b_Xv], chain=True)
                k.op("act", lambda e: e.activation(out=vc_l[0:nr, t * 65:t * 65 + 64], in_=Xv[0:nr, 0:64], func=AF.Identity), r=[b_Xv], w=[b_vc])

            for c in range(NCH):
                q0 = c * 512
                ctx.chunk(q0)
                fmt, b_fmt = fm_t.next()
                k.dma(fmt[:].rearrange("p (t j) -> p t j", t=4), fm_d[q0:q0 + 512, :].rearrange("(t p) j -> p t j", p=128), w=[b_fmt])
                gt, b_gt = g_t.next()
                k.dma(gt[64:65, :].rearrange("p (r n) -> p r n", r=6), gates_d[u:u + 1, :, q0:q0 + 512], w=[b_gt])
                tmax = min(NCT - 1, (q0 + 480) // 2048)
                cms = []
                for t in range(tmax + 1):
                    cm, b_cm = cm_t.next()
                    k.op("pool", lambda e: e.tensor_scalar(out=cm[:], in0=iota[:], scalar1=cthr[:, t * NCH + c:t * NCH + c + 1], scalar2=NEG, op0=ALU.is_lt, op1=ALU.mult),
                         r=[b_iota, b_cthr], w=[b_cm])
                    cms.append((cm, b_cm))
                for hh in range(4):
                    base = (hh % 2) * 64
                    qt_, b_qt = (q01, b_q01) if hh < 2 else (q23, b_q23)
                    qs = qt_[base:base + 64, q0:q0 + 512]
                    if hh < 2:
                        Oc, b_Oc = PSO.next()
                    for t in range(tmax + 1):
                        nr = min(128, ncmp + 1 - 128 * t)
                        A, b_A = PSA.next()
                        cm, b_cm = cms[t]
                        k.op("pe", lambda e: e.matmul(A[0:nr, :], lhsT=kcT2[base:base + 64, t * 128:t * 128 + nr], rhs=qs, start=True, stop=False), r=[b_kc, b_qt], w=[b_A])
                        k.op("pe", lambda e: e.matmul(A[0:nr, :], lhsT=ident[0:nr, 0:nr], rhs=cm[0:nr, :], start=False, stop=True), r=[b_id, b_cm], w=[b_A], chain=True)
                        P, b_P = ctx.score_to_p(A, b_A, sl[0:nr, hh:hh + 1], b_sl, cposs[0:nr, hh * NCT + t:hh * NCT + t + 1], b_cposs, np_=nr)
                        if hh < 2:
                            k.op("pe", lambda e: e.matmul(Oc[0:65, :], lhsT=vc_l[0:nr, t * 65:t * 65 + 65], rhs=P[0:nr, :], start=(t == 0), stop=(t == tmax)),
                                 r=[b_vc, b_P], w=[b_Oc], chain=True)
                        for qt in range(4):
                            ip = IMP[qt // 2]
                            co = (qt % 2) * (NJ + 1)
                            k.op("pe", lambda e: e.matmul(ip[:, co:co + NJ + 1], lhsT=P[0:nr, qt * 128:(qt + 1) * 128], rhs=ovl[0:nr, t * (NJ + 1):(t + 1) * (NJ + 1)],
                                                          start=(t == 0), stop=(t == tmax)), r=[b_P, b_ovl], w=[b_IMP[qt // 2]], chain=True)
                    for qt in range(4):
                        ip = IMP[qt // 2]
                        co = (qt % 2) * (NJ + 1)
                        k.op("dve", lambda e: e.tensor_scalar(out=rden[:, qt:qt + 1], in0=ip[:, co + NJ:co + NJ + 1], scalar1=1e-30, scalar2=None, op0=ALU.max),
                             r=[b_IMP[qt // 2]], w=[b_rden])
                    k.op("dve", lambda e: e.reciprocal(out=rden[:], in_=rden[:]), r=[b_rden], w=[b_rden])
                    for qt in range(4):
                        ip = IMP[qt // 2]
                        co = (qt % 2) * (NJ + 1)
                        if hh == 0:
                            k.op("dve", lambda e: e.tensor_scalar(out=acc_imp[:, qt * NJ:(qt + 1) * NJ], in0=ip[:, co:co + NJ], scalar1=rden[:, qt:qt + 1], scalar2=None, op0=ALU.mult),
                                 r=[b_IMP[qt // 2], b_rden], w=[b_acc])
                        else:
                            k.op("dve", lambda e: e.scalar_tensor_tensor(out=acc_imp[:, qt * NJ:(qt + 1) * NJ], in0=ip[:, co:co + NJ], scalar=rden[:, qt:qt + 1],
                                                                         in1=acc_imp[:, qt * NJ:(qt + 1) * NJ], op0=ALU.mult, op1=ALU.add),
                                 r=[b_IMP[qt // 2], b_rden, b_acc], w=[b_acc])
                    if hh < 2:
                        combine_branch(Oc, b_Oc, hh, 0, gt, b_gt)
                T, b_T = PST.next()
                for qt in range(4):
                    sc, b_sc = sc_t.next()
                    k.op("dve", lambda e: e.tensor_tensor(out=sc[:], in0=acc_imp[:, qt * NJ:(qt + 1) * NJ], in1=fmt[:, qt * NJ:(qt + 1) * NJ], op=ALU.add), r=[b_acc, b_fmt], w=[b_sc])
                    wk_, b_wk = wk_t.next()
                    k.op("dve", lambda e: e.max(out=m8[:, 0:8], in_=sc[:]), r=[b_sc], w=[b_m8])
                    k.op("dve", lambda e: e.match_replace(out=wk_[:], in_to_replace=m8[:, 0:8], in_values=sc[:], imm_value=-1e30), r=[b_sc, b_m8], w=[b_wk])
                    k.op("dve", lambda e: e.max(out=m8[:, 8:16], in_=wk_[:]), r=[b_wk], w=[b_m8])
                    sn, b_sn = sn_t.next()
                    k.op("dve", lambda e: e.tensor_scalar(out=sn[:], in0=sc[:], scalar1=m8[:, 15:16], scalar2=NEG, op0=ALU.is_lt, op1=ALU.mult), r=[b_sc, b_m8], w=[b_sn])
                    k.op("pe", lambda e: e.matmul(T[0:NJ, qt * 128:(qt + 1) * 128], lhsT=sn[:], rhs=ident[:], start=True, stop=True), r=[b_sn, b_id], w=[b_T])
                k.op("act", lambda e: e.activation(out=selnegT[0:NJ, :], in_=T[0:NJ, :], func=AF.Identity), r=[b_T], w=[b_snT])
                for i in range(2):
                    base = i * 64
                    qs = q01[base:base + 64, q0:q0 + 512]
                    zt, b_zt = z_t.next()
                    k.dma(zt[:], nz_d[u, base:base + 64, q0:q0 + 512], w=[b_zt])
                    for br in (1, 2):
                        O, b_O = PSO.next()
                        kT, b_kT, vL, b_vL = (ksel, b_ksel, vsel, b_vsel) if br == 1 else (kwin, b_kwin, vwin, b_vwin)
                        kts = list(range(0, 4 * c + 4)) if br == 1 else list(range(max(0, 4 * c - 4), 4 * c + 4))
                        pend = []
                        for idx, kt in enumerate(kts):
                            d = kt - 4 * c
                            A, b_A = PSA.next()
                            k.op("pe", lambda e: e.matmul(A[:], lhsT=kT[base:base + 64, kt * 128:(kt + 1) * 128], rhs=qs, start=True, stop=False), r=[b_kT, b_q01], w=[b_A])
                            if br == 1:
                                k.op("pe", lambda e: e.matmul(A[:], lhsT=efull[0:NJ, kt * 128:(kt + 1) * 128], rhs=selnegT[0:NJ, :], start=False, stop=(d < 0)),
                                     r=[b_ef, b_snT], w=[b_A], chain=True)
                                if d >= 0:
                                    off = 384 - 128 * d
                                    k.op("pe", lambda e: e.matmul(A[:], lhsT=ident[:], rhs=tns[:, off:off + 512], start=False, stop=True), r=[b_id, b_tns], w=[b_A], chain=True)
                            else:
                                off = 384 - 128 * d
                                k.op("pe", lambda e: e.matmul(A[:], lhsT=ident[:], rhs=wst[:, off:off + 512], start=False, stop=True), r=[b_id, b_wst], w=[b_A], chain=True)
                            P, b_P = ctx.score_to_p(A, b_A, sl[:, i:i + 1], b_sl, pks[:, i * NT + kt:i * NT + kt + 1], b_pks)
                            pend.append((P, b_P, kt, idx))
                            if len(pend) > 1:
                                P2, b_P2, kt2, idx2 = pend.pop(0)
                                k.op("pe", lambda e: e.matmul(O[0:65, :], lhsT=vL[:, kt2 * 65:kt2 * 65 + 65], rhs=P2[:], start=(idx2 == 0), stop=False), r=[b_vL, b_P2], w=[b_O], chain=True)
                        P2, b_P2, kt2, idx2 = pend.pop(0)
                        k.op("pe", lambda e: e.matmul(O[0:65, :], lhsT=vL[:, kt2 * 65:kt2 * 65 + 65], rhs=P2[:], start=(idx2 == 0), stop=True), r=[b_vL, b_P2], w=[b_O], chain=True)
                        combine_branch(O, b_O, i, br, gt, b_gt)
                    ot, b_ot = o_t.next()
                    a, b_a = accO[i]
                    k.op("pool", lambda e: e.tensor_tensor(out=ot[:], in0=a[:], in1=zt[:], op=ALU.mult), r=[b_a, b_zt], w=[b_ot])
                    k.dma(mixT[u, base:base + 64, q0:q0 + 512], ot[:], r=[b_ot], q="pool")
        k.finish()
    return nc


NSA_UNITS = [(0, 0), (0, 1), (1, 0), (1, 1), (2, 0), (2, 1)]


def nsa_in_maps(P, j, fmb, fmf, tmb, S, nbatch):
    cst = _nsa_consts(S)
    slopes = _alibi_slopes(12)
    wk = np.asarray(P["nsa_cmp_wk"][j], np.float32).reshape(32, 64, 64).transpose(1, 0, 2).reshape(64, 2048)
    wv = np.asarray(P["nsa_cmp_wv"][j], np.float32).reshape(32, 64, 64).transpose(1, 0, 2).reshape(64, 2048)
    wcmp = np.ascontiguousarray(np.concatenate([wk, wv], axis=0))
    peT = np.ascontiguousarray(np.concatenate([np.asarray(P["nsa_cmp_pe_k"][j], np.float32).T, np.asarray(P["nsa_cmp_pe_v"][j], np.float32).T], axis=0))
    kng = np.ascontiguousarray(np.asarray(P["nsa_kn"][j], np.float32).reshape(64, 1))
    maps = []
    for core in range(2 * nbatch):
        b, hf = core // 2, core % 2
        units = NSA_UNITS[hf * 3:(hf + 1) * 3]
        F, G, T = fmb[b], fmf[b], tmb[b]
        m = {k_: [] for k_ in ("nq", "ksel", "kwin", "ckcv", "vsel", "vwin", "gates", "nz", "slopes")}
        for (g, p) in units:
            hp = [2 * p, 2 * p + 1] + [x for x in range(4) if x not in (2 * p, 2 * p + 1)]
            m["nq"].append(np.concatenate([F[(g * 4 + h) * 64:(g * 4 + h + 1) * 64] for h in hp], axis=0))
            sk = F[768 + g * 64:768 + (g + 1) * 64]
            wk_ = F[960 + g * 64:960 + (g + 1) * 64]
            m["ksel"].append(np.concatenate([sk, sk], axis=0))
            m["kwin"].append(np.concatenate([wk_, wk_], axis=0))
            m["ckcv"].append(np.concatenate([F[2688 + g * 64:2688 + (g + 1) * 64], F[2880 + g * 64:2880 + (g + 1) * 64]], axis=0))
            m["vsel"].append(_v_layout_multi([T[:, g * 64:(g + 1) * 64]]))
            m["vwin"].append(_v_layout_multi([T[:, 192 + g * 64:192 + (g + 1) * 64]]))
            m["gates"].append(np.stack([G[1024 + (g * 4 + hp[i]) * 3 + br] for i in range(2) for br in range(3)], axis=0))
            m["nz"].append(np.concatenate([G[(g * 4 + hp[i]) * 64:(g * 4 + hp[i] + 1) * 64] for i in range(2)], axis=0))
            m["slopes"].append(np.broadcast_to(np.array([slopes[g * 4 + h] for h in hp], np.float32)[None, :], (128, 4)))
        mm = {k_: np.ascontiguousarray(np.stack(v)) for k_, v in m.items()}
        mm.update(wcmp=wcmp, peT=peT, kng=kng, pos=np.asarray(P["positions"][:S], np.int32), poskcol=_poskcol(P["positions"], S),
                  fm=cst["fm"], ovl=cst["ovl"], cthr=cst["cthr"], iota=cst["iota"], efull=cst["efull"], wst=cst["wst"], tns=cst["tns"],
                  ident_in=np.eye(128, dtype=np.float32).astype(NPBF))
        maps.append(mm)
    return maps


def dil_in_maps(P, fmb, fmf, tmb, S, nbatch):
    slopes = _alibi_slopes(12).reshape(3, 4)
    strips = np.concatenate([_dil_strip(W, dl) for W, dl in DIL_CFG], axis=1)
    maps = []
    for core in range(2 * nbatch):
        b, hf = core // 2, core % 2
        hs = [2 * hf, 2 * hf + 1]
        F, G, T = fmb[b], fmf[b], tmb[b]
        m = {
            "dq": np.ascontiguousarray(np.concatenate([F[1152 + (g * 4 + h) * 64:1152 + (g * 4 + h + 1) * 64] for g in range(3) for h in hs], axis=0)),
            "dk": np.ascontiguousarray(np.concatenate([F[1920 + (g * 4 + h) * 64:1920 + (g * 4 + h + 1) * 64] for g in range(3) for h in hs], axis=0)),
            "dv": _v_layout_multi([T[:, 384 + (g * 4 + h) * 64:384 + (g * 4 + h + 1) * 64] for g in range(3) for h in hs]),
            "dz": np.ascontiguousarray(np.concatenate([G[768 + h * 64:768 + (h + 1) * 64] for h in hs], axis=0)),
            "slopes": np.ascontiguousarray(np.broadcast_to(np.array([slopes[g, h] for g in range(3) for h in hs], np.float32)[None, :], (128, 6))),
            "pos": np.asarray(P["positions"][:S], np.int32), "poskcol": _poskcol(P["positions"], S),
            "dstrips": strips.astype(NPBF), "ident_in": np.eye(128, dtype=np.float32).astype(NPBF),
        }
        maps.append(m)
    return maps


def odd_mix_assemble(res_nsa, res_dil, S, nbatch):
    out = []
    for b in range(nbatch):
        mix = np.zeros((1024, S), NPBF)
        for hf in range(2):
            core = 2 * b + hf
            r = res_nsa[core]["mixT"]
            for ui, (g, p) in enumerate(NSA_UNITS[hf * 3:(hf + 1) * 3]):
                for i in range(2):
                    h = g * 4 + 2 * p + i
                    mix[h * 64:(h + 1) * 64] = r[ui, i * 64:(i + 1) * 64]
            rd = res_dil[core]["mixT"]
            for i in range(2):
                h = 2 * hf + i
                mix[768 + h * 64:768 + (h + 1) * 64] = rd[i * 64:(i + 1) * 64]
        out.append(mix)
    return out


_PROGS = {}


def _prog(key, builder):
    if key not in _PROGS:
        _PROGS[key] = builder()
    return _PROGS[key]


def _run(nc, maps):
    res = run_bass_kernel_spmd(nc, maps, core_ids=list(range(len(maps))))
    return res.results


def even_in_maps(fmb, fmf, tmb, S, nbatch):
    strips, sqc = _attn_consts()
    maps = []
    for core in range(2 * nbatch):
        b, hf = core // 2, core % 2
        F, G, T = fmb[b], fmf[b], tmb[b]
        krot = F[2304:2336]
        mk = np.concatenate([np.concatenate([F[1792 + h * 64:1792 + (h + 1) * 64], krot], axis=0) for h in range(4 * hf, 4 * hf + 4)], axis=0)
        maps.append({
            "sbq": np.ascontiguousarray(F[hf * 256:(hf + 1) * 256]),
            "sbk": np.ascontiguousarray(F[512 + hf * 256:512 + (hf + 1) * 256]),
            "sbv": _v_layout(np.ascontiguousarray(T[:, hf * 256:(hf + 1) * 256]), 4),
            "sbz": np.ascontiguousarray(G[hf * 256:(hf + 1) * 256]),
            "mq": np.ascontiguousarray(F[1024 + hf * 384:1024 + (hf + 1) * 384]),
            "mk": np.ascontiguousarray(mk),
            "mv": _v_layout(np.ascontiguousarray(T[:, 512 + hf * 256:512 + (hf + 1) * 256]), 4),
            "mz": np.ascontiguousarray(G[512 + hf * 256:512 + (hf + 1) * 256]),
            "strips_in": strips, "sqc_in": sqc,
        })
    return maps


def even_mix_assemble(res, S, nbatch):
    out = []
    for b in range(nbatch):
        mix = np.zeros((1024, S), NPBF)
        for hf in range(2):
            r = res[2 * b + hf]["mixT"]
            mix[hf * 256:(hf + 1) * 256] = r[0:256]
            mix[512 + hf * 256:512 + (hf + 1) * 256] = r[256:512]
        out.append(mix)
    return out


def _gather_proj(res, nbatch, has_next, has_prev):
    fmb = fmf = tmb = xT = None
    if has_next:
        fmb = [np.concatenate([res[2 * b]["fmb"], res[2 * b + 1]["fmb"]], axis=1) for b in range(nbatch)]
        fmf = [np.concatenate([res[2 * b]["fmf"], res[2 * b + 1]["fmf"]], axis=1) for b in range(nbatch)]
        tmb = [np.concatenate([res[2 * b]["tmb"], res[2 * b + 1]["tmb"]], axis=0) for b in range(nbatch)]
    if has_prev:
        xT = np.stack([np.concatenate([res[2 * b]["xT_new"], res[2 * b + 1]["xT_new"]], axis=1) for b in range(nbatch)])
    return fmb, fmf, tmb, xT


def kernel(**inputs):
    P = {k_: np.asarray(v) for k_, v in inputs.items()}
    x = P["x"]
    nbatch, S, _ = x.shape
    depth = P["ada_w"].shape[0]
    Tc = S // 2
    ncores = 2 * nbatch
    xT = np.ascontiguousarray(np.transpose(x, (0, 2, 1)))
    mix = None
    for layer in range(depth + 1):
        prev = layer - 1 if layer > 0 else None
        nxt = layer if layer < depth else None
        kind = None if nxt is None else ("even" if nxt % 2 == 0 else "odd")
        nc = _prog(("proj", Tc, prev is not None, kind), lambda: build_proj(Tc, prev is not None, kind))
        res = _run(nc, proj_in_maps(P, prev, nxt, xT, mix, Tc, ncores))
        fmb, fmf, tmb, xT_new = _gather_proj(res, nbatch, nxt is not None, prev is not None)
        if xT_new is not None:
            xT = xT_new
        if nxt is None:
            break
        if kind == "even":
            nc = _prog(("even", S), lambda: build_attn_even(S))
            r = _run(nc, even_in_maps(fmb, fmf, tmb, S, nbatch))
            mix = even_mix_assemble(r, S, nbatch)
        else:
            nc = _prog(("nsa", S), lambda: build_attn_nsa(S))
            rn = _run(nc, nsa_in_maps(P, nxt // 2, fmb, fmf, tmb, S, nbatch))
            nc = _prog(("dil", S), lambda: build_attn_dil(S))
            rd = _run(nc, dil_in_maps(P, fmb, fmf, tmb, S, nbatch))
            mix = odd_mix_assemble(rn, rd, S, nbatch)
    return np.ascontiguousarray(np.transpose(xT, (0, 2, 1))).astype(np.float32)
```
